# Optimizing a Trainium2 kernel written in Bass

```python
import jax, jax.numpy as jnp
from jax import lax
import numpy as np

D_MODEL = 2048
BATCH = 8
SEQ = 4096
DEPTH = 4

GRID_W = 64
CTX_LEN = 256
Q_BLOCK = 128
ROPE_THETA = 10000.0
EPS = 1e-6
A_HEADS = 8
A_KV_HEADS = 2
A_GROUP = A_HEADS // A_KV_HEADS
A_HEAD_DIM = 128
B_HEADS = 8
B_Q_RANK = 512
B_KV_RANK = 256
B_NOPE_DIM = 128
B_ROPE_DIM = 64
B_V_DIM = 128
A_SCALE = A_HEAD_DIM ** -0.5
B_SCALE = (B_NOPE_DIM + B_ROPE_DIM) ** -0.5
ATTN_SIZES = (A_HEADS * A_HEAD_DIM, A_KV_HEADS * A_HEAD_DIM, A_KV_HEADS * A_HEAD_DIM,
              B_Q_RANK, B_KV_RANK, B_ROPE_DIM)
ATTN_IN = sum(ATTN_SIZES)
ATTN_OUT = A_HEADS * A_HEAD_DIM + B_HEADS * B_V_DIM
SC_WIDTH = D_MODEL
CONV_W = 3
D_FF = 256 * ((8 * D_MODEL // 3 + 255) // 256)
N_ATTN_LAYERS = (DEPTH + 1) // 2
N_CONV_LAYERS = DEPTH // 2

kernel_name = "hybrid_dit_gqa_mla_shortconv_convffn"


def rms_norm(x, w):
    xf = x.astype(jnp.float32)
    y = xf * lax.rsqrt(jnp.mean(xf * xf, axis=-1, keepdims=True) + EPS)
    return (y * w.astype(jnp.float32)).astype(x.dtype)


def adaln(cvec, w, b):
    m = jax.nn.silu(cvec) @ w + b
    if m.ndim == 2:
        m = m[:, None, :]
    return jnp.split(m, 6, axis=-1)


def modulate(h, shift, scale):
    return h * (1.0 + scale) + shift


def axial_rope_tables(rows, rot_dim, dtype):
    r = jnp.repeat(jnp.arange(rows, dtype=jnp.float32), GRID_W)
    col = jnp.tile(jnp.arange(GRID_W, dtype=jnp.float32), rows)
    quarter = rot_dim // 4
    inv = ROPE_THETA ** (-jnp.arange(quarter, dtype=jnp.float32) / quarter)
    ar = r[:, None] * inv
    ac = col[:, None] * inv
    ang = jnp.concatenate([ar, ar, ac, ac], axis=-1)
    return jnp.cos(ang).astype(dtype), jnp.sin(ang).astype(dtype)


def apply_rope(x, cos, sin):
    shape = (cos.shape[0],) + (1,) * (x.ndim - 3) + (cos.shape[-1],)
    cos = cos.reshape(shape)
    sin = sin.reshape(shape)
    a, b, c, d = jnp.split(x, 4, axis=-1)
    rot = jnp.concatenate([-b, a, -d, c], axis=-1)
    return x * cos + rot * sin


def dwconv(x, w):
    s = x.shape[1]
    pad = CONV_W // 2
    xp = jnp.pad(x, ((0, 0), (pad, CONV_W - 1 - pad), (0, 0)))
    out = xp[:, 0:s] * w[0]
    for j in range(1, CONV_W):
        out = out + xp[:, j:j + s] * w[j]
    return out


def blocked_attention(q, k, v, scale):
    bn, s, kh, g, dq = q.shape
    nb = s // Q_BLOCK
    qb = q.reshape(bn, nb, Q_BLOCK, kh, g, dq).transpose(1, 0, 2, 3, 4, 5)

    def one_block(qblk):
        sc = jnp.einsum('bqhgd,bthd->bhgqt', qblk, k,
                        preferred_element_type=jnp.float32) * scale
        p = jax.nn.softmax(sc, axis=-1).astype(v.dtype)
        return jnp.einsum('bhgqt,bthd->bqhgd', p, v)

    o = lax.map(one_block, qb)
    return o.transpose(1, 0, 2, 3, 4, 5).reshape(bn, s, kh * g * v.shape[-1])


def attn_project(h, w_in, q_norm_a, k_norm_a, q_norm_b, kv_norm_b, w_uq, w_ukv,
                 rope_a, rope_b, with_queries):
    bn, s, _ = h.shape
    proj = h @ w_in
    idx, acc = [], 0
    for sz in ATTN_SIZES[:-1]:
        acc += sz
        idx.append(acc)
    qa, ka, va, cq, ckv, kr = jnp.split(proj, idx, axis=-1)
    ka = rms_norm(ka.reshape(bn, s, A_KV_HEADS, A_HEAD_DIM), k_norm_a)
    va = va.reshape(bn, s, A_KV_HEADS, A_HEAD_DIM)
    kv = (rms_norm(ckv, kv_norm_b) @ w_ukv).reshape(bn, s, B_HEADS, B_NOPE_DIM + B_V_DIM)
    k_nope, vb = jnp.split(kv, [B_NOPE_DIM], axis=-1)
    if rope_a is not None:
        ka = apply_rope(ka, *rope_a)
        kr = apply_rope(kr, *rope_b)
    kb = jnp.concatenate(
        [k_nope, jnp.broadcast_to(kr[:, :, None, :], (bn, s, B_HEADS, B_ROPE_DIM))], axis=-1)
    if not with_queries:
        return None, ka, va, None, kb, vb
    qa = rms_norm(qa.reshape(bn, s, A_KV_HEADS, A_GROUP, A_HEAD_DIM), q_norm_a)
    qb = (rms_norm(cq, q_norm_b) @ w_uq).reshape(bn, s, B_HEADS, B_NOPE_DIM + B_ROPE_DIM)
    q_nope, q_rope = jnp.split(qb, [B_NOPE_DIM], axis=-1)
    if rope_a is not None:
        qa = apply_rope(qa, *rope_a)
        q_rope = apply_rope(q_rope, *rope_b)
    qb = jnp.concatenate([q_nope, q_rope], axis=-1)[:, :, :, None, :]
    return qa, ka, va, qb, kb, vb


def short_conv_mixer(h, w_in, conv_w, w_out):
    b, cg, v = jnp.split(h @ w_in, 3, axis=-1)
    return (b * dwconv(cg * v, conv_w)) @ w_out


def conv_ffn(h, w_up, conv_w, w_down):
    g, u = jnp.split(dwconv(h @ w_up, conv_w), 2, axis=-1)
    return (jax.nn.silu(g) * u) @ w_down


def setup_inputs(seed: int = 0) -> dict:
    key = jax.random.key(seed)
    ks = jax.random.split(key, 23)
    f32 = jnp.float32

    def nrm(k, shape, scale):
        return jax.random.normal(k, shape, f32) * scale

    def gain(k, shape):
        return 1.0 + 0.02 * jax.random.normal(k, shape, f32)

    D = D_MODEL
    return {
        "x": nrm(ks[0], (BATCH, SEQ, D), 1.0),
        "c": nrm(ks[1], (BATCH, D), 1.0),
        "ctx": nrm(ks[2], (BATCH, CTX_LEN, D), 1.0),
        "c_ctx": nrm(ks[3], (D,), 1.0),
        "w_ada": nrm(ks[4], (DEPTH, D, 6 * D), 0.5 * D ** -0.5),
        "b_ada": nrm(ks[5], (DEPTH, 6 * D), 0.02),
        "norm_mix": gain(ks[6], (DEPTH, D)),
        "norm_ffn": gain(ks[7], (DEPTH, D)),
        "attn_w_in": nrm(ks[8], (N_ATTN_LAYERS, D, ATTN_IN), D ** -0.5),
        "attn_q_norm": gain(ks[9], (N_ATTN_LAYERS, A_HEAD_DIM)),
        "attn_k_norm": gain(ks[10], (N_ATTN_LAYERS, A_HEAD_DIM)),
        "mla_q_norm": gain(ks[11], (N_ATTN_LAYERS, B_Q_RANK)),
        "mla_kv_norm": gain(ks[12], (N_ATTN_LAYERS, B_KV_RANK)),
        "mla_w_uq": nrm(ks[13], (N_ATTN_LAYERS, B_Q_RANK, B_HEADS * (B_NOPE_DIM + B_ROPE_DIM)),
                        B_Q_RANK ** -0.5),
        "mla_w_ukv": nrm(ks[14], (N_ATTN_LAYERS, B_KV_RANK, B_HEADS * (B_NOPE_DIM + B_V_DIM)),
                         B_KV_RANK ** -0.5),
        "attn_w_o": nrm(ks[15], (N_ATTN_LAYERS, ATTN_OUT, D), ATTN_OUT ** -0.5),
        "sc_w_in": nrm(ks[16], (N_CONV_LAYERS, D, 3 * SC_WIDTH), D ** -0.5),
        "sc_conv": nrm(ks[17], (N_CONV_LAYERS, CONV_W, SC_WIDTH), CONV_W ** -0.5),
        "sc_w_out": nrm(ks[18], (N_CONV_LAYERS, SC_WIDTH, D), SC_WIDTH ** -0.5),
        "ffn_w_up": nrm(ks[19], (DEPTH, D, 2 * D_FF), D ** -0.5),
        "ffn_conv": nrm(ks[20], (DEPTH, CONV_W, 2 * D_FF), CONV_W ** -0.5),
        "ffn_w_down": nrm(ks[21], (DEPTH, D_FF, D), D_FF ** -0.5),
        "final_norm": gain(ks[22], (D,)),
    }


def reference(x, c, ctx, c_ctx, w_ada, b_ada, norm_mix, norm_ffn, attn_w_in, attn_q_norm,
              attn_k_norm, mla_q_norm, mla_kv_norm, mla_w_uq, mla_w_ukv, attn_w_o,
              sc_w_in, sc_conv, sc_w_out, ffn_w_up, ffn_conv, ffn_w_down, final_norm):
    ROWS = x.shape[1] // GRID_W
    rope_a = axial_rope_tables(ROWS, A_HEAD_DIM, x.dtype)
    rope_b = axial_rope_tables(ROWS, B_ROPE_DIM, x.dtype)
    xc = ctx
    for l in range(DEPTH):
        later_attn = any(j % 2 == 0 for j in range(l + 1, DEPTH))
        is_attn = (l % 2 == 0)
        sh, sc, g, shf, scf, gf = adaln(c, w_ada[l], b_ada[l])
        h = modulate(rms_norm(x, norm_mix[l]), sh, sc)
        if is_attn or later_attn:
            csh, csc, cg, cshf, cscf, cgf = adaln(c_ctx, w_ada[l], b_ada[l])
            hc = modulate(rms_norm(xc, norm_mix[l]), csh, csc)
        if is_attn:
            i = l // 2
            prm = (attn_w_in[i], attn_q_norm[i], attn_k_norm[i], mla_q_norm[i], mla_kv_norm[i],
                   mla_w_uq[i], mla_w_ukv[i])
            qa_c, ka_c, va_c, qb_c, kb_c, vb_c = attn_project(hc, *prm, None, None, later_attn)
            qa_l, ka_l, va_l, qb_l, kb_l, vb_l = attn_project(h, *prm, rope_a, rope_b, True)
            ka_all = jnp.concatenate([ka_c, ka_l], axis=1)
            va_all = jnp.concatenate([va_c, va_l], axis=1)
            kb_all = jnp.concatenate([kb_c, kb_l], axis=1)
            vb_all = jnp.concatenate([vb_c, vb_l], axis=1)
            y = jnp.concatenate([blocked_attention(qa_l, ka_all, va_all, A_SCALE),
                                 blocked_attention(qb_l, kb_all, vb_all, B_SCALE)],
                                axis=-1) @ attn_w_o[i]
            if later_attn:
                yc = jnp.concatenate([blocked_attention(qa_c, ka_c, va_c, A_SCALE),
                                      blocked_attention(qb_c, kb_c, vb_c, B_SCALE)],
                                     axis=-1) @ attn_w_o[i]
        else:
            i = l // 2
            y = short_conv_mixer(h, sc_w_in[i], sc_conv[i], sc_w_out[i])
            if later_attn:
                yc = short_conv_mixer(hc, sc_w_in[i], sc_conv[i], sc_w_out[i])
        x = x + g * y
        hf = modulate(rms_norm(x, norm_ffn[l]), shf, scf)
        x = x + gf * conv_ffn(hf, ffn_w_up[l], ffn_conv[l], ffn_w_down[l])
        if later_attn:
            xc = xc + cg * yc
            hcf = modulate(rms_norm(xc, norm_ffn[l]), cshf, cscf)
            xc = xc + cgf * conv_ffn(hcf, ffn_w_up[l], ffn_conv[l], ffn_w_down[l])
    return rms_norm(x, final_norm)
```

```python
import numpy as np
from contextlib import ExitStack
import concourse.bass as bass
import concourse.mybir as mybir
from concourse.bass_utils import run_bass_kernel_spmd

F32 = mybir.dt.float32
BF16 = mybir.dt.bfloat16
AF = mybir.ActivationFunctionType
ALU = mybir.AluOpType

D = 2048
S = 4096
CT = 256
NKEY = S + CT
DFF = 5632
NF = DFF // 128
TV = 456
NTL = 9
NMAX = 458
EPS = 1e-6
A_SCALE = 128 ** -0.5
B_SCALE = 192 ** -0.5
NWB = 5
WBE = 6144
NDS = 40
NCS = 8
ENGS = ("pe", "act", "dve", "sp", "pool")


class Reg:
    __slots__ = ("w", "rd", "rdma")

    def __init__(self):
        self.w = None
        self.rd = {}
        self.rdma = []


class Op:
    __slots__ = ("eng", "fn", "deps", "dma", "dsem", "dcnt", "signal", "cnt")


class Sched:
    def __init__(self):
        self.ops = {e: [] for e in ENGS}
        self.nd = [0, 0]
        self.dma_last = [[None] * NDS, [None] * NCS]
        self.dma_cnt = [[0] * NDS, [0] * NCS]
        self.bar = {}

    def add(self, eng, fn, reads=(), writes=(), dma=0):
        o = Op()
        o.eng = eng
        o.fn = fn
        o.dma = dma
        o.signal = False
        o.cnt = 0
        deps = []
        for r in reads:
            if r.w is not None:
                deps.append(r.w)
        for w in writes:
            if w.w is not None:
                deps.append(w.w)
            deps.extend(w.rd.values())
            deps.extend(w.rdma)
        if eng in self.bar:
            deps.extend(self.bar.pop(eng))
        if dma:
            pool = dma - 1
            n = NDS if pool == 0 else NCS
            k = self.nd[pool] % n
            self.nd[pool] += 1
            if self.dma_last[pool][k] is not None:
                deps.append(self.dma_last[pool][k])
            self.dma_last[pool][k] = o
            self.dma_cnt[pool][k] += 16
            o.dsem = (pool, k)
            o.dcnt = self.dma_cnt[pool][k]
        seen = set()
        dl = []
        for d in deps:
            if id(d) in seen:
                continue
            seen.add(id(d))
            if (not d.dma) and (not dma) and d.eng == "pe" and eng == "pe":
                continue
            dl.append(d)
            if not d.dma:
                d.signal = True
        o.deps = dl
        for r in reads:
            if dma:
                r.rdma.append(o)
            else:
                r.rd[eng] = o
        for w in writes:
            w.w = o
            w.rd = {}
            w.rdma = []
        self.ops[eng].append(o)
        return o

    def barrier(self):
        deps = []
        for e in ENGS:
            for o in reversed(self.ops[e]):
                if not o.dma:
                    deps.append(o)
                    break
        deps.extend(o for o in self.dma_last[0] if o is not None)
        for e in ENGS:
            self.bar[e] = list(deps) + self.bar.get(e, [])

    def finalize(self):
        for e in ENGS:
            c = 0
            for o in self.ops[e]:
                if o.signal and not o.dma:
                    c += 1
                    o.cnt = c

    def replay(self, eng, h, esem, dsem):
        waited = {}
        for o in self.ops[eng]:
            for d in o.deps:
                if d.dma:
                    key = d.dsem
                    val = d.dcnt
                    sem = dsem[d.dsem[0]][d.dsem[1]]
                else:
                    key = d.eng
                    val = d.cnt
                    sem = esem[d.eng]
                if waited.get(key, 0) < val:
                    h.wait_ge(sem, val)
                    waited[key] = val
            ins = o.fn(h)
            if o.dma:
                ins.then_inc(dsem[o.dsem[0]][o.dsem[1]], 16)
            elif o.signal:
                ins.then_inc(esem[o.eng], 1)


def make_tiles():
    tiles = []
    for i in range(NTL):
        t0 = TV * i - 1
        hi = min(NMAX, S - t0)
        tiles.append(dict(kind=0, i=i, t0=t0, n=NMAX, lo=1 if i == 0 else 0, hi=hi, olo=1, ohi=min(NMAX - 1, hi)))
    ctx = dict(kind=1, i=0, t0=-1, n=CT + 2, lo=1, hi=CT + 1, olo=1, ohi=CT + 1)
    return tiles, ctx


def build(nlayers=4):
    nc = bass.Bass("TRN2", target_bir_lowering=False)
    sch = Sched()
    NOREG = ()

    def din(name, shape, dt=F32):
        return nc.dram_tensor(name, list(shape), dt, kind="ExternalInput").ap()

    def dscr(name, shape, dt=BF16):
        return nc.dram_tensor(name, list(shape), dt, kind="Internal").ap()

    xT = din("xT", [D, S])
    cT = din("ctxT", [D, CT])
    cvec = din("cvec", [128, 32])
    w_ada = din("w_ada", [4 * 96, 128, 2048])
    bada = din("bada", [128, 4 * 96 * 2])
    nmix_d = din("nmix", [128, 64])
    nffn_d = din("nffn", [128, 64])
    fnorm_d = din("fnorm", [128, 16])
    fconv_d = din("fconv", [128, 4 * 3 * 88])
    sconv_d = din("sconv", [128, 2 * 3 * 16])
    hn_d = din("hnorm", [128, 2 * 8])
    ropeA = din("ropeA", [2, 128, S])
    ropeB = din("ropeB", [2, 128, S])
    consts_d = din("consts", [128, 7 * 128])
    wsrc = {
        "w_in": din("w_in", [2 * 19, 128, 2048]),
        "w_uq": din("w_uq", [2 * 12, 128, 512]),
        "w_ukv": din("w_ukv", [2 * 16, 128, 256]),
        "w_o": din("w_o", [2 * 16, 128, 2048]),
        "sc_in": din("sc_in", [2 * 48, 128, 2048]),
        "sc_out": din("sc_out", [2 * 16, 128, 2048]),
        "f_up": din("f_up", [4 * 88, 128, 2048]),
        "f_dn": din("f_dn", [4 * 16, 128, 5632]),
    }
    wsrc["ada"] = w_ada
    outT = nc.dram_tensor("outT", [D, S], F32, kind="ExternalOutput").ap()

    wbf = {k: dscr(k + "_bf", v.shape) for k, v in wsrc.items()}
    wreg = {k: [Reg() for _ in range(v.shape[0])] for k, v in wsrc.items()}
    X = [dscr("X0", [D, S], F32), dscr("X1", [D, S], F32)]
    XC = [dscr("XC0", [D, CT], F32), dscr("XC1", [D, CT], F32)]
    QA = dscr("QA", [8 * 128, S])
    QBN = dscr("QBN", [8 * 128, S])
    QBR = dscr("QBR", [4 * 128, S])
    QAc = dscr("QAc", [8 * 128, CT])
    QBNc = dscr("QBNc", [8 * 128, CT])
    QBRc = dscr("QBRc", [4 * 128, CT])
    KA = dscr("KA", [2 * 128, NKEY])
    KBN = dscr("KBN", [8 * 128, NKEY])
    KR = dscr("KR", [128, NKEY])
    VA = dscr("VA", [NKEY, 256])
    VB = dscr("VB", [NKEY, 1024])
    OT = dscr("OT", [D, S])
    OTc = dscr("OTc", [D, CT])

    es = ExitStack()
    with es:
        def sb(name, shape, dt):
            return es.enter_context(nc.sbuf_tensor(name, list(shape), dt))

        xt = sb("xt", [128, 16, NMAX], F32)
        ht = sb("ht", [128, 16, NMAX], BF16)
        sq = sb("sq", [128, 4, 512], BF16)
        rstd = sb("rstd", [128, 512], F32)
        tmp = sb("tmp", [128, 8, 512], F32)
        big = sb("big", [128, 20736], BF16)
        wb = sb("wb", [128, NWB, WBE], BF16)
        ropA = sb("ropA", [128, 2, NMAX], F32)
        ropB = sb("ropB", [128, 2, NMAX], F32)
        cst_f = sb("cst_f", [128, 7 * 128], F32)
        cst = sb("cst", [128, 7, 128], BF16)
        cv_f = sb("cv_f", [128, 32], F32)
        cv_b = sb("cv_b", [128, 32], BF16)
        modv = sb("modv", [128, 4 * 96, 2], F32)
        bada_s = sb("bada_s", [128, 4 * 96 * 2], F32)
        nmix = sb("nmix_s", [128, 64], F32)
        nffn = sb("nffn_s", [128, 64], F32)
        fnorm = sb("fnorm_s", [128, 16], F32)
        fconv = sb("fconv_s", [128, 4 * 3 * 88], F32)
        sconv = sb("sconv_s", [128, 2 * 3 * 16], F32)
        hn = sb("hn_s", [128, 16], F32)
        hns = sb("hns_s", [128, 16], F32)
        AB = sb("AB", [128, 4 * 2 * 2, 16], F32)
        cstg = sb("cstg", [128, 2, 4096], BF16)
        ps = [es.enter_context(nc.psum_tensor("ps%d" % i, [128, 512], F32)) for i in range(8)]

        esem = {e: es.enter_context(nc.semaphore("se_" + e)) for e in ("pe", "act", "dve", "pool")}
        esem["sp"] = esem["pe"]
        dsem = [[es.enter_context(nc.semaphore("sd%d" % i)) for i in range(NDS)],
                [es.enter_context(nc.semaphore("sc%d" % i)) for i in range(NCS)]]

        xt_r = [Reg() for _ in range(16)]
        ht_r = [Reg() for _ in range(16)]
        sq_r = [Reg() for _ in range(4)]
        rstd_r = Reg()
        tmp_r = [Reg() for _ in range(8)]
        wb_r = [Reg() for _ in range(NWB)]
        ps_r = [Reg() for _ in range(8)]
        ropA_r = Reg()
        ropB_r = Reg()
        cst_r = Reg()
        vec_r = Reg()
        rot = {"sq": 0, "tmp": 0, "wb": 0, "ps": 0, "big": 0, "vst": 0, "cstg": 0}
        cstg_r = [Reg(), Reg()]

        def nxt(k, n):
            v = rot[k]
            rot[k] = (v + 1) % n
            return v

        def MM(out, lhsT, rhs, start, stop, reads, writes):
            sch.add("pe", lambda e: e.matmul(out, lhsT, rhs, start=start, stop=stop), reads, writes)

        def ACT(out, in_, func, reads, writes, bias=0.0, scale=1.0):
            sch.add("act", lambda e: e.activation(out=out, in_=in_, func=func, bias=bias, scale=scale), reads, writes)

        def TS(out, in0, s1, s2, op0, op1, reads, writes):
            if s2 is None:
                sch.add("dve", lambda e: e.tensor_scalar(out, in0, s1, None, op0), reads, writes)
            else:
                sch.add("dve", lambda e: e.tensor_scalar(out, in0, s1, s2, op0, op1), reads, writes)

        def STT(out, in0, scalar, in1, op0, op1, reads, writes):
            sch.add("dve", lambda e: e.scalar_tensor_tensor(out, in0, scalar, in1, op0, op1), reads, writes)

        def TT(out, in0, in1, op, reads, writes):
            sch.add("dve", lambda e: e.tensor_tensor(out, in0, in1, op), reads, writes)

        def RECIP(out, in_, reads, writes):
            sch.add("dve", lambda e: e.reciprocal(out, in_), reads, writes)

        def COPY(out, in_, reads, writes):
            sch.add("dve", lambda e: e.tensor_copy(out, in_), reads, writes)

        def MEMSET(ap, v, writes):
            sch.add("dve", lambda e: e.memset(ap, v), (), writes)

        def DMA(eng, out, in_, reads, writes, pool=1):
            sch.add(eng, lambda e: e.dma_start(out=out, in_=in_), reads, writes, dma=pool)

        cast_q = []

        def queue_cast(name, s0, s1, step):
            w = wsrc[name].shape[2]
            if w > 4096:
                h = w // 2
                for a in range(s0, s1):
                    cast_q.append((name, a, a + 1, 0, h))
                    cast_q.append((name, a, a + 1, h, w))
                return
            step = max(1, 4096 // w)
            for a in range(s0, s1, step):
                b = min(a + step, s1)
                cast_q.append((name, a, b, 0, w))

        def pump(nbytes):
            done = 0
            while cast_q and done < nbytes:
                name, a, b, c0, c1 = cast_q.pop(0)
                src = wsrc[name]
                ns, w = b - a, c1 - c0
                k = nxt("cstg", 2)
                stage = cstg[:, k, 0:ns * w].rearrange("p (s f) -> p s f", s=ns)
                DMA("pool", stage, src[a:b, :, c0:c1].rearrange("s p f -> p s f"), (), [cstg_r[k]], pool=2)
                DMA("pool", wbf[name][a:b, :, c0:c1].rearrange("s p f -> p s f"), stage, [cstg_r[k]], wreg[name][a:b], pool=2)
                done += ns * 128 * w * 4

        def layer_casts(l):
            i = l // 2
            if l >= 1:
                queue_cast("ada", l * 96, l * 96 + 96, 2)
            if l % 2 == 0:
                queue_cast("w_in", i * 19, i * 19 + 19, 2)
                queue_cast("w_uq", i * 12, i * 12 + 12, 12)
                queue_cast("w_ukv", i * 16, i * 16 + 16, 8)
                queue_cast("w_o", i * 16, i * 16 + 16, 2)
            else:
                queue_cast("sc_in", i * 48, i * 48 + 48, 2)
                queue_cast("sc_out", i * 16, i * 16 + 16, 2)
            queue_cast("f_up", l * 88, l * 88 + 88, 2)
            queue_cast("f_dn", l * 16, l * 16 + 16, 1)

        def load_w(name, s0, ns):
            i = nxt("wb", NWB)
            per = wbf[name].shape[2]
            dst = wb[:, i, 0:ns * per].rearrange("p (s f) -> p s f", s=ns)
            DMA("sp", dst, wbf[name][s0:s0 + ns, :, :].rearrange("s p f -> p s f"), wreg[name][s0:s0 + ns], [wb_r[i]])
            return i

        DMA("sp", cst_f[:, :], consts_d[:, :], (), [cst_r])
        COPY(cst[:, :, :], cst_f[:, :].rearrange("p (a b) -> p a b", a=7), [cst_r], [cst_r])
        for dst_, src_ in ((bada_s, bada), (nmix, nmix_d), (nffn, nffn_d), (fnorm, fnorm_d), (fconv, fconv_d),
                           (sconv, sconv_d), (cv_f, cvec), (hn, hn_d)):
            DMA("sp", dst_[:, :], src_[:, :], (), [vec_r])
        ACT(cv_b[:, :], cv_f[:, :], AF.Silu, [vec_r], [vec_r])
        TS(hns[:, :], hn[:, :], A_SCALE, None, ALU.mult, None, [vec_r], [vec_r])
        MEMSET(xt[:, :, :], 0.0, xt_r)
        MEMSET(ht[:, :, :], 0.0, ht_r)
        MEMSET(tmp[:, :, :], 0.0, tmp_r)
        MEMSET(sq[:, :, :], 0.0, sq_r)
        MEMSET(big[:, :], 0.0, ())
        MEMSET(rstd[:, :], 0.0, [rstd_r])
        sch.barrier()

        layer_casts(0)
        pump(30 << 20)

        ADA_PS = 7
        ada_state = {}

        def ada_slab(l, jj):
            if l == 0:
                i = nxt("wb", NWB)
                dst = wb[:, i, 0:2048]
                DMA("pool", dst, w_ada[l * 96 + jj, :, :], (), [wb_r[i]], pool=2)
            else:
                i = load_w("ada", l * 96 + jj, 1)
            for kc in range(16):
                MM(ps[ADA_PS][:, 2 * jj:2 * jj + 2], wb[:, i, kc * 128:kc * 128 + 128],
                   cv_b[:, 2 * kc:2 * kc + 2], kc == 0, kc == 15, [wb_r[i], vec_r], [ps_r[ADA_PS]])

        def ada_finish(l):
            TT(modv[:, l * 96:(l + 1) * 96, :].rearrange("p a b -> p (a b)"), ps[ADA_PS][:, 0:192],
               bada_s[:, l * 192:(l + 1) * 192], ALU.add, [ps_r[ADA_PS], vec_r], [vec_r])
            sch.barrier()
            for which in range(2):
                for s_ in range(2):
                    nw = (nmix if which == 0 else nffn)[:, l * 16:(l + 1) * 16]
                    sc0 = l * 96 + which * 48 + 16
                    STT(AB[:, (l * 2 + which) * 2 + s_, :], modv[:, sc0:sc0 + 16, s_], 1.0, nw, ALU.add, ALU.mult,
                        [vec_r], [vec_r])
            sch.barrier()

        def ada_step(l):
            jj = ada_state.get(l, 0)
            if l < nlayers and jj < 96:
                ada_slab(l, jj)
                ada_state[l] = jj + 1

        if nlayers > 0:
            for jj in range(96):
                ada_step(0)
            ada_finish(0)

        def modcol(l, j, c, s):
            return modv[:, l * 96 + j * 16 + c, s:s + 1]

        def Acol(l, which, c, s):
            return AB[:, (l * 2 + which) * 2 + s, c:c + 1]

        tiles, ctile = make_tiles()

        def load_x(t, src):
            a, b = t["lo"], t["hi"]
            DMA("sp", xt[:, :, a:b], src.rearrange("(c p) t -> p c t", p=128)[:, :, t["t0"] + a:t["t0"] + b], (), xt_r)

        def store_x(t, dst):
            a, b = t["olo"], t["ohi"]
            DMA("act", dst.rearrange("(c p) t -> p c t", p=128)[:, :, t["t0"] + a:t["t0"] + b], xt[:, :, a:b], xt_r, ())

        def stats(n, ones_idx, srcs):
            pb = nxt("ps", 7)
            for k, (ap, regs) in enumerate(srcs):
                j = nxt("sq", 4)
                ACT(sq[:, j, :n], ap, AF.Square, regs, [sq_r[j]])
                MM(ps[pb][:, :n], cst[:, ones_idx, :], sq[:, j, :n], k == 0, k == len(srcs) - 1,
                   [sq_r[j], cst_r], [ps_r[pb]])
            ACT(rstd[:, :n], ps[pb][:, :n], AF.Sqrt, [ps_r[pb]], [rstd_r], bias=EPS)
            RECIP(rstd[:, :n], rstd[:, :n], [rstd_r], [rstd_r])

        def norm_mod(t, l, which, zero_pad):
            n = t["n"]
            s = t["kind"]
            stats(n, 0, [(xt[:, c, :n], [xt_r[c]]) for c in range(16)])
            for c in range(16):
                k = nxt("tmp", 8)
                TT(tmp[:, k, :n], xt[:, c, :n], rstd[:, :n], ALU.mult, [xt_r[c], rstd_r], [tmp_r[k]])
                ACT(ht[:, c, :n], tmp[:, k, :n], AF.Identity, [tmp_r[k], vec_r], [ht_r[c]],
                    bias=modcol(l, which * 3, c, s), scale=Acol(l, which, c, s))
            if zero_pad:
                if t["lo"] > 0:
                    MEMSET(ht[:, :, 0:t["lo"]], 0.0, ht_r)
                if t["hi"] < n:
                    MEMSET(ht[:, :, t["hi"]:n], 0.0, ht_r)

        def resid(t, c, pb, n2, off, l, gj):
            s = t["kind"]
            STT(xt[:, c, off:off + n2], ps[pb][:, :n2], modcol(l, gj, c, s), xt[:, c, off:off + n2],
                ALU.mult, ALU.add, [ps_r[pb], xt_r[c], vec_r], [xt_r[c]])

        z_r = [Reg() for _ in range(16)]

        def z_ap(c, n2):
            return big[:, c * NMAX:c * NMAX + n2]

        def outproj(t, l, wname, s0, rhs_fn, rhs_regs, n2, off):
            for c in range(16):
                wi = load_w(wname, s0 + c, 1)
                pb = nxt("ps", 7)
                for kc in range(16):
                    MM(ps[pb][:, :n2], wb[:, wi, kc * 128:kc * 128 + 128], rhs_fn(kc), kc == 0, kc == 15,
                       [wb_r[wi], rhs_regs[kc]], [ps_r[pb]])
                resid(t, c, pb, n2, off, l, 2)

        def sconv_tile(l, t, src, dst):
            i = l // 2
            n = t["n"]
            n2 = n - 2
            load_x(t, src)
            norm_mod(t, l, 0, True)
            for c in range(16):
                pump(1 << 19)
                wi = load_w("sc_in", i * 48 + c * 3, 3)
                pbs = []
                for j in range(3):
                    pb = nxt("ps", 7)
                    pbs.append(pb)
                    for kc in range(16):
                        MM(ps[pb][:, :n], wb[:, wi, j * 2048 + kc * 128:j * 2048 + kc * 128 + 128], ht[:, kc, :n],
                           kc == 0, kc == 15, [wb_r[wi], ht_r[kc]], [ps_r[pb]])
                kv = nxt("tmp", 8)
                ACT(tmp[:, kv, :n], ps[pbs[2]][:, :n], AF.Identity, [ps_r[pbs[2]]], [tmp_r[kv]])
                TT(tmp[:, kv, :n], ps[pbs[1]][:, :n], tmp[:, kv, :n], ALU.mult, [ps_r[pbs[1]], tmp_r[kv]], [tmp_r[kv]])
                k = nxt("tmp", 8)
                w = [sconv[:, (i * 3 + tap) * 16 + c:(i * 3 + tap) * 16 + c + 1] for tap in range(3)]
                TS(tmp[:, k, :n2], tmp[:, kv, 0:n2], w[0], None, ALU.mult, None, [tmp_r[kv], vec_r], [tmp_r[k]])
                STT(tmp[:, k, :n2], tmp[:, kv, 1:n2 + 1], w[1], tmp[:, k, :n2], ALU.mult, ALU.add,
                    [tmp_r[kv], vec_r, tmp_r[k]], [tmp_r[k]])
                STT(tmp[:, k, :n2], tmp[:, kv, 2:n2 + 2], w[2], tmp[:, k, :n2], ALU.mult, ALU.add,
                    [tmp_r[kv], vec_r, tmp_r[k]], [tmp_r[k]])
                TT(z_ap(c, n2), ps[pbs[0]][:, 1:n2 + 1], tmp[:, k, :n2], ALU.mult, [ps_r[pbs[0]], tmp_r[k]], [z_r[c]])
            outproj(t, l, "sc_out", i * 16, lambda kc: z_ap(kc, n2), z_r, n2, 1)
            store_x(t, dst)

        act_r = [Reg() for _ in range(NF)]

        def act_ap(f, n2):
            return big[:, f * NMAX:f * NMAX + n2]

        def ffn_tile(l, t, src, dst):
            n = t["n"]
            n2 = n - 2
            load_x(t, src)
            if l % 2 == 0:
                a, b = t["lo"], t["hi"]
                osrc = OT if t["kind"] == 0 else OTc
                otv = big[:, 0:16 * NMAX].rearrange("p (c n) -> p c n", c=16)
                DMA("sp", otv[:, :, a:b], osrc.rearrange("(c p) t -> p c t", p=128)[:, :, t["t0"] + a:t["t0"] + b],
                    (), act_r[0:16])
                outproj(t, l, "w_o", (l // 2) * 16, lambda kc: big[:, kc * NMAX:kc * NMAX + n], act_r, n, 0)
            norm_mod(t, l, 1, True)
            for f in range(NF):
                if f % 4 == 3:
                    ada_step(l + 1)
                if f % 4 == 1:
                    pump(3 << 20)
                wi = load_w("f_up", l * 88 + 2 * f, 2)
                tk = []
                for j in range(2):
                    pb = nxt("ps", 7)
                    for kc in range(16):
                        MM(ps[pb][:, :n], wb[:, wi, j * 2048 + kc * 128:j * 2048 + kc * 128 + 128], ht[:, kc, :n],
                           kc == 0, kc == 15, [wb_r[wi], ht_r[kc]], [ps_r[pb]])
                    k = nxt("tmp", 8)
                    ch = j * NF + f
                    w = [fconv[:, (l * 3 + tap) * 88 + ch:(l * 3 + tap) * 88 + ch + 1] for tap in range(3)]
                    TS(tmp[:, k, :n2], ps[pb][:, 0:n2], w[0], None, ALU.mult, None, [ps_r[pb], vec_r], [tmp_r[k]])
                    STT(tmp[:, k, :n2], ps[pb][:, 1:n2 + 1], w[1], tmp[:, k, :n2], ALU.mult, ALU.add,
                        [ps_r[pb], vec_r, tmp_r[k]], [tmp_r[k]])
                    STT(tmp[:, k, :n2], ps[pb][:, 2:n2 + 2], w[2], tmp[:, k, :n2], ALU.mult, ALU.add,
                        [ps_r[pb], vec_r, tmp_r[k]], [tmp_r[k]])
                    if j == 0:
                        ACT(tmp[:, k, :n2], tmp[:, k, :n2], AF.Silu, [tmp_r[k]], [tmp_r[k]])
                    tk.append(k)
                TT(act_ap(f, n2), tmp[:, tk[0], :n2], tmp[:, tk[1], :n2], ALU.mult,
                   [tmp_r[tk[0]], tmp_r[tk[1]]], [act_r[f]])
            for c in range(16):
                wi = load_w("f_dn", l * 16 + c, 1)
                pb = nxt("ps", 7)
                for kc in range(NF):
                    MM(ps[pb][:, :n2], wb[:, wi, kc * 128:kc * 128 + 128], act_ap(kc, n2), kc == 0, kc == NF - 1,
                       [wb_r[wi], act_r[kc]], [ps_r[pb]])
                resid(t, c, pb, n2, 1, l, 5)
            if l == nlayers - 1:
                stats(n, 0, [(xt[:, c, :n], [xt_r[c]]) for c in range(16)])
                for c in range(16):
                    STT(xt[:, c, :n], xt[:, c, :n], fnorm[:, c:c + 1], rstd[:, :n], ALU.mult, ALU.mult,
                        [xt_r[c], rstd_r, vec_r], [xt_r[c]])
                store_x(t, outT)
            else:
                store_x(t, dst)

        def attnout_tile(l, t, src, dst):
            i = l // 2
            n = t["n"]
            a, b = t["lo"], t["hi"]
            pump(14 << 20)
            load_x(t, src)
            osrc = OT if t["kind"] == 0 else OTc
            otv = big[:, 0:16 * NMAX].rearrange("p (c n) -> p c n", c=16)
            DMA("sp", otv[:, :, a:b], osrc.rearrange("(c p) t -> p c t", p=128)[:, :, t["t0"] + a:t["t0"] + b], (), z_r)
            outproj(t, l, "w_o", i * 16, lambda kc: big[:, kc * NMAX:kc * NMAX + n], z_r, n, 0)
            store_x(t, dst)

        stg_r = [Reg() for _ in range(8)]
        BIGQ = 8 * 512

        def stg(k, n):
            return big[:, k * 512:k * 512 + n]

        cqn_r = [Reg() for _ in range(6)]

        def cqn(k, n):
            return big[:, BIGQ + k * 512:BIGQ + k * 512 + n]

        vst_r = [Reg() for _ in range(2)]

        def vst(k):
            return big[:, BIGQ + 6 * 512 + k * 1024:BIGQ + 6 * 512 + (k + 1) * 1024]

        def store_feat(t, dst_lat, dst_ctx, row0, k, lat_off):
            a, b = t["olo"], t["ohi"]
            if t["kind"] == 0:
                d = dst_lat[row0:row0 + 128, lat_off + t["t0"] + a:lat_off + t["t0"] + b]
            else:
                d = dst_ctx[row0:row0 + 128, t["t0"] + a:t["t0"] + b]
            DMA("pool", d, stg(k, NMAX)[:, a:b], [stg_r[k]], ())

        def rope_tail(n, src_k, sq_j, perm_idx, rop, rop_r, outk):
            pb = nxt("ps", 7)
            MM(ps[pb][:, :n], cst[:, perm_idx, :], sq[:, sq_j, :n], True, True, [sq_r[sq_j], cst_r], [ps_r[pb]])
            k1 = nxt("tmp", 8)
            TT(tmp[:, k1, :n], tmp[:, src_k, :n], rop[:, 0, :n], ALU.mult, [tmp_r[src_k], rop_r], [tmp_r[k1]])
            k2 = nxt("tmp", 8)
            TT(tmp[:, k2, :n], ps[pb][:, :n], rop[:, 1, :n], ALU.mult, [ps_r[pb], rop_r], [tmp_r[k2]])
            TT(stg(outk, n), tmp[:, k1, :n], tmp[:, k2, :n], ALU.add, [tmp_r[k1], tmp_r[k2]], [stg_r[outk]])

        def proj_fm(wi, woff, KC, rhs_fn, rhs_regs, n):
            pb = nxt("ps", 7)
            for kc in range(KC):
                MM(ps[pb][:, :n], wb[:, wi, woff + kc * 128:woff + kc * 128 + 128], rhs_fn(kc), kc == 0, kc == KC - 1,
                   [wb_r[wi], rhs_regs[kc]], [ps_r[pb]])
            return pb

        def pipeline(tasks):
            nt = len(tasks)
            for step in range(nt + 2):
                for st_ in range(3):
                    k = step - st_
                    if 0 <= k < nt and st_ < len(tasks[k]):
                        tasks[k][st_]()

        def qkv_tile(l, t, src, with_q):
            i = l // 2
            n = t["n"]
            lat = t["kind"] == 0
            koff = CT if lat else 0
            load_x(t, src)
            norm_mod(t, l, 0, False)
            if lat:
                a, b = t["lo"], t["hi"]
                DMA("sp", ropA[:, :, a:b], ropeA.rearrange("k p t -> p k t")[:, :, t["t0"] + a:t["t0"] + b], (), [ropA_r])
                DMA("sp", ropB[:, :, a:b], ropeB.rearrange("k p t -> p k t")[:, :, t["t0"] + a:t["t0"] + b], (), [ropB_r])
            hfn = lambda kc: ht[:, kc, :n]
            hb = i * 8

            def head_task(slab, gain_ap, dst_lat, dst_ctx, row0, lat_off):
                st = {}

                def s1():
                    wi = load_w("w_in", i * 19 + slab, 1)
                    st["pb"] = proj_fm(wi, 0, 16, hfn, ht_r, n)

                def s2():
                    pb = st["pb"]
                    stats(n, 1, [(ps[pb][:, :n], [ps_r[pb]])])
                    k = nxt("tmp", 8)
                    st["k"] = k
                    TT(tmp[:, k, :n], ps[pb][:, :n], rstd[:, :n], ALU.mult, [ps_r[pb], rstd_r], [tmp_r[k]])
                    if not lat:
                        ko = nxt("big", 8)
                        st["ko"] = ko
                        ACT(stg(ko, n), tmp[:, k, :n], AF.Identity, [tmp_r[k], vec_r], [stg_r[ko]], scale=gain_ap)
                    else:
                        ACT(tmp[:, k, :n], tmp[:, k, :n], AF.Identity, [tmp_r[k], vec_r], [tmp_r[k]], scale=gain_ap)
                        j = nxt("sq", 4)
                        st["j"] = j
                        ACT(sq[:, j, :n], tmp[:, k, :n], AF.Identity, [tmp_r[k]], [sq_r[j]])

                def s3():
                    if lat:
                        ko = nxt("big", 8)
                        st["ko"] = ko
                        rope_tail(n, st["k"], st["j"], 5, ropA, ropA_r, ko)
                    store_feat(t, dst_lat, dst_ctx, row0, st["ko"], lat_off)

                return [s1, s2, s3]

            def latent_tasks(c0, ncn, ones_idx, g0, k0):
                sh = {"ks": []}
                tasks = []
                for c in range(ncn):
                    def mk(c):
                        st = {}

                        def s1():
                            wi = load_w("w_in", i * 19 + c0 + c, 1)
                            st["pb"] = proj_fm(wi, 0, 16, hfn, ht_r, n)

                        def s2():
                            pb = st["pb"]
                            if c == 0:
                                sh["pbs"] = nxt("ps", 7)
                            pbs = sh["pbs"]
                            k = nxt("tmp", 8)
                            sh["ks"].append(k)
                            ACT(tmp[:, k, :n], ps[pb][:, :n], AF.Identity, [ps_r[pb]], [tmp_r[k]])
                            j = nxt("sq", 4)
                            ACT(sq[:, j, :n], ps[pb][:, :n], AF.Square, [ps_r[pb]], [sq_r[j]])
                            MM(ps[pbs][:, :n], cst[:, ones_idx, :], sq[:, j, :n], c == 0, c == ncn - 1,
                               [sq_r[j], cst_r], [ps_r[pbs]])
                            if c == ncn - 1:
                                ACT(rstd[:, :n], ps[pbs][:, :n], AF.Sqrt, [ps_r[pbs]], [rstd_r], bias=EPS)
                                RECIP(rstd[:, :n], rstd[:, :n], [rstd_r], [rstd_r])
                                for cc in range(ncn):
                                    kk = sh["ks"][cc]
                                    TT(tmp[:, kk, :n], tmp[:, kk, :n], rstd[:, :n], ALU.mult, [tmp_r[kk], rstd_r], [tmp_r[kk]])
                                    ACT(cqn(k0 + cc, n), tmp[:, kk, :n], AF.Identity, [tmp_r[kk], vec_r], [cqn_r[k0 + cc]],
                                        scale=hn[:, hb + g0 + cc:hb + g0 + cc + 1])

                        return [s1, s2]
                    tasks.append(mk(c))
                return tasks

            def simple_task(proj, scale, do_rope, rop, rop_r, dst_lat, dst_ctx, row0, lat_off):
                st = {}

                def s1():
                    st["pb"] = proj()

                def s2():
                    pb = st["pb"]
                    if do_rope:
                        kt = nxt("tmp", 8)
                        st["k"] = kt
                        ACT(tmp[:, kt, :n], ps[pb][:, :n], AF.Identity, [ps_r[pb]], [tmp_r[kt]], scale=scale)
                        j = nxt("sq", 4)
                        st["j"] = j
                        ACT(sq[:, j, :n], ps[pb][:, :n], AF.Identity, [ps_r[pb]], [sq_r[j]], scale=scale)
                    else:
                        ko = nxt("big", 8)
                        st["ko"] = ko
                        ACT(stg(ko, n), ps[pb][:, :n], AF.Identity, [ps_r[pb]], [stg_r[ko]], scale=scale)

                def s3():
                    if do_rope:
                        ko = nxt("big", 8)
                        st["ko"] = ko
                        rope_tail(n, st["k"], st["j"], 6, rop, rop_r, ko)
                    store_feat(t, dst_lat, dst_ctx, row0, st["ko"], lat_off)

                return [s1, s2, s3]

            g1 = []
            if with_q:
                for h in range(8):
                    g1.append(head_task(h, hns[:, hb:hb + 1], QA, QAc, h * 128, 0))
            for h in range(2):
                g1.append(head_task(8 + h, hn[:, hb + 1:hb + 2], KA, KA, h * 128, koff))
            if with_q:
                g1 += latent_tasks(10, 4, 2, 2, 0)
            g1 += latent_tasks(14, 2, 3, 6, 4)

            def kr_proj():
                wi = load_w("w_in", i * 19 + 16, 1)
                return proj_fm(wi, 0, 16, hfn, ht_r, n)
            g1.append(simple_task(kr_proj, 1.0, lat, ropB, ropB_r, KR, KR, 0, koff))

            blocks = []
            c0 = t["olo"]
            while c0 < t["ohi"]:
                m = min(128, t["ohi"] - c0)
                blocks.append((c0, m))
                c0 += m
            wv = {}

            def va_task(c0, m, first):
                st = {}
                row0 = koff + t["t0"] + c0

                def s1():
                    if first:
                        wv["a"] = load_w("w_in", i * 19 + 17, 2)
                    wva = wv["a"]
                    pb = nxt("ps", 7)
                    st["pb"] = pb
                    for j in range(2):
                        for kc in range(16):
                            MM(ps[pb][0:m, j * 128:j * 128 + 128], ht[:, kc, c0:c0 + m],
                               wb[:, wva, j * 2048 + kc * 128:j * 2048 + kc * 128 + 128], kc == 0, kc == 15,
                               [wb_r[wva], ht_r[kc]], [ps_r[pb]])

                def s2():
                    pb = st["pb"]
                    vk = nxt("vst", 2)
                    ACT(vst(vk)[0:m, 0:256], ps[pb][0:m, 0:256], AF.Identity, [ps_r[pb]], [vst_r[vk]])
                    DMA("pool", VA[row0:row0 + m, :], vst(vk)[0:m, 0:256], [vst_r[vk]], ())

                return [s1, s2]

            for bi, (c0, m) in enumerate(blocks):
                g1.append(va_task(c0, m, bi == 0))
            pipeline(g1)

            g2 = []
            if with_q:
                def mkq(off):
                    def f():
                        if "q" not in wv:
                            wv["q"] = load_w("w_uq", i * 12, 12)
                        return proj_fm(wv["q"], off, 4, lambda kc: cqn(kc, n), cqn_r[0:4], n)
                    return f
                for h in range(8):
                    g2.append(simple_task(mkq(h * 512), B_SCALE, False, None, None, QBN, QBNc, h * 128, 0))
                for hp in range(4):
                    g2.append(simple_task(mkq((8 + hp) * 512), B_SCALE, lat, ropB, ropB_r, QBR, QBRc, hp * 128, 0))

            def mkk(h):
                def f():
                    if "k" not in wv:
                        wv["k"] = load_w("w_ukv", i * 16, 8)
                    return proj_fm(wv["k"], h * 256, 2, lambda kc: cqn(4 + kc, n), cqn_r[4:6], n)
                return f
            for h in range(8):
                g2.append(simple_task(mkk(h), 1.0, False, None, None, KBN, KBN, h * 128, koff))

            def vb_task(c0, m):
                st = {}
                row0 = koff + t["t0"] + c0

                def s1():
                    if "v" not in wv:
                        wv["v"] = load_w("w_ukv", i * 16 + 8, 8)
                    wvb = wv["v"]
                    st["pbs"] = []
                    for half in range(2):
                        pb = nxt("ps", 7)
                        st["pbs"].append(pb)
                        for j in range(4):
                            h = half * 4 + j
                            for kc in range(2):
                                MM(ps[pb][0:m, j * 128:j * 128 + 128], cqn(4 + kc, n)[:, c0:c0 + m],
                                   wb[:, wvb, h * 256 + kc * 128:h * 256 + kc * 128 + 128], kc == 0, kc == 1,
                                   [wb_r[wvb], cqn_r[4 + kc]], [ps_r[pb]])

                def s2():
                    vk = nxt("vst", 2)
                    for half in range(2):
                        pb = st["pbs"][half]
                        ACT(vst(vk)[0:m, half * 512:half * 512 + 512], ps[pb][0:m, 0:512], AF.Identity,
                            [ps_r[pb]], [vst_r[vk]])
                    DMA("pool", VB[row0:row0 + m, :], vst(vk)[0:m, :], [vst_r[vk]], ())

                return [s1, s2]

            for (c0, m) in blocks:
                g2.append(vb_task(c0, m))
            pipeline(g2)

        K_r, KR_r, V_r = Reg(), Reg(), Reg()
        Kb = big[:, 0:NKEY]
        KRb = big[:, NKEY:2 * NKEY]
        Vb = big[:, 2 * NKEY:3 * NKEY].rearrange("p (c d) -> p c d", c=34)
        QO = 3 * NKEY
        q_r = [Reg() for _ in range(2)]
        qr_r = [Reg() for _ in range(2)]
        p_r = [Reg() for _ in range(4)]
        o_r = [Reg() for _ in range(4)]

        def qb(k):
            return big[:, QO + k * 512:QO + (k + 1) * 512]

        def qrb(k):
            return big[:, QO + 1024 + k * 512:QO + 1024 + (k + 1) * 512]

        def pbuf(k):
            return big[:, QO + 2048 + k * 512:QO + 2048 + (k + 1) * 512]

        def obuf(k):
            return big[:, QO + 4096 + k * 512:QO + 4096 + (k + 1) * 512]

        att = {"q": 0, "p": 0, "o": 0, "s": 0, "acc": 0}

        dacc_r = [Reg() for _ in range(4)]
        dsum_r = [Reg() for _ in range(2)]

        def PADD(eng, out, in0, in1, reads, writes):
            sch.add(eng, lambda e: e.tensor_tensor(out, in0, in1, ALU.add), reads, writes)

        def PCOPY(eng, out, in_, reads, writes):
            sch.add(eng, lambda e: e.tensor_copy(out, in_), reads, writes)

        def q_load(blk):
            qi = att["q"]
            att["q"] ^= 1
            blk["qi"] = qi
            nq = blk["nq"]
            DMA("sp", qb(qi)[:, :nq], blk["qsrc"], (), [q_r[qi]])
            if blk["qrsrc"] is not None:
                DMA("sp", qrb(qi)[:, :nq], blk["qrsrc"], (), [qr_r[qi]])

        def attn_block(blk, nxt_blk):
            qsrc, qrsrc, half, nq, kcs, odst = (blk["qsrc"], blk["qrsrc"], blk["half"], blk["nq"], blk["kcs"], blk["odst"])
            if "qi" not in blk:
                q_load(blk)
            qi = blk["qi"]
            ab = att["acc"]
            att["acc"] ^= 1
            po, pd = 4 + ab, 6 + ab
            prev = None

            def pv(kc, pi, first, last):
                MM(ps[po][:, :nq], Vb[:, kc, :], pbuf(pi)[:, :nq], first, last, [V_r, p_r[pi]], [ps_r[po]])
                MM(ps[pd][:, :nq], cst[:, 4, :], pbuf(pi)[:, :nq], first, last, [cst_r, p_r[pi]], [ps_r[pd]])

            for idx, kc in enumerate(kcs):
                sbk = att["s"]
                att["s"] = (sbk + 1) % 4
                MM(ps[sbk][:, :nq], Kb[:, kc * 128:kc * 128 + 128], qb(qi)[:, :nq], True, qrsrc is None,
                   [K_r, q_r[qi]], [ps_r[sbk]])
                if qrsrc is not None:
                    MM(ps[sbk][:, :nq], KRb[half * 64:half * 64 + 64, kc * 128:kc * 128 + 128],
                       qrb(qi)[half * 64:half * 64 + 64, :nq], False, True, [KR_r, qr_r[qi]], [ps_r[sbk]])
                pi = att["p"]
                att["p"] = (pi + 1) % 4
                ACT(pbuf(pi)[:, :nq], ps[sbk][:, :nq], AF.Exp, [ps_r[sbk]], [p_r[pi]])
                if prev is not None:
                    pv(prev[0], prev[1], prev[2], False)
                prev = (kc, pi, idx == 0)
                if idx == 1 and nxt_blk is not None:
                    q_load(nxt_blk)
            pv(prev[0], prev[1], prev[2], True)
            k = nxt("tmp", 8)
            RECIP(tmp[:, k, :nq], ps[pd][:, :nq], [ps_r[pd]], [tmp_r[k]])
            oi = att["o"]
            att["o"] = (oi + 1) % 4
            TT(obuf(oi)[:, :nq], ps[po][:, :nq], tmp[:, k, :nq], ALU.mult, [ps_r[po], tmp_r[k]], [o_r[oi]])
            DMA("sp", odst, obuf(oi)[:, :nq], [o_r[oi]], ())

        def attn_core(l, ctx_q):
            allk = list(range(34))
            blocks = []

            def qtiles(Qd, Qdc, row0, Qr=None, Qrc=None, rrow0=0, half=0, ochunk=0, kvload=None):
                first = True
                for qt in range(8):
                    blocks.append(dict(qsrc=Qd[row0:row0 + 128, qt * 512:(qt + 1) * 512],
                                       qrsrc=None if Qr is None else Qr[rrow0:rrow0 + 128, qt * 512:(qt + 1) * 512],
                                       half=half, nq=512, kcs=allk,
                                       odst=OT[ochunk * 128:ochunk * 128 + 128, qt * 512:(qt + 1) * 512],
                                       newkv=kvload if first else None))
                    first = False
                if ctx_q:
                    blocks.append(dict(qsrc=Qdc[row0:row0 + 128, :], qrsrc=None if Qr is None else Qrc[rrow0:rrow0 + 128, :],
                                       half=half, nq=CT, kcs=[0, 1], odst=OTc[ochunk * 128:ochunk * 128 + 128, :], newkv=None))

            def kv_a(kv):
                def f():
                    DMA("sp", Kb, KA[kv * 128:kv * 128 + 128, :], (), [K_r])
                    DMA("sp", Vb, VA[:, kv * 128:kv * 128 + 128].rearrange("(c p) d -> p c d", p=128), (), [V_r])
                return f

            def kv_b(h):
                def f():
                    if h == 0:
                        DMA("sp", KRb, KR[:, :], (), [KR_r])
                    DMA("sp", Kb, KBN[h * 128:h * 128 + 128, :], (), [K_r])
                    DMA("sp", Vb, VB[:, h * 128:h * 128 + 128].rearrange("(c p) d -> p c d", p=128), (), [V_r])
                return f

            for kv in range(2):
                for g in range(4):
                    h = kv * 4 + g
                    qtiles(QA, QAc, h * 128, ochunk=h, kvload=kv_a(kv) if g == 0 else None)
            for h in range(8):
                qtiles(QBN, QBNc, h * 128, QBR, QBRc, (h // 2) * 128, h % 2, 8 + h, kvload=kv_b(h))
            for bi, blk in enumerate(blocks):
                pump(2 << 20)
                if blk["newkv"] is not None:
                    blk["newkv"]()
                attn_block(blk, blocks[bi + 1] if bi + 1 < len(blocks) else None)

        def final_tile(t, src):
            n = t["n"]
            load_x(t, src)
            stats(n, 0, [(xt[:, c, :n], [xt_r[c]]) for c in range(16)])
            for c in range(16):
                STT(xt[:, c, :n], xt[:, c, :n], fnorm[:, c:c + 1], rstd[:, :n], ALU.mult, ALU.mult,
                    [xt_r[c], rstd_r, vec_r], [xt_r[c]])
            store_x(t, outT)

        cur, curc = xT, cT
        nb, nbc = 0, 0

        def run_pass(fn, l, with_ctx, advance=True):
            nonlocal cur, curc, nb, nbc
            if with_ctx:
                fn(l, ctile, curc, XC[nbc])
            for t in tiles:
                fn(l, t, cur, X[nb])
            sch.barrier()
            if with_ctx:
                curc = XC[nbc]
                nbc ^= 1
            cur = X[nb]
            nb ^= 1

        for l in range(nlayers):
            later_attn = any(j % 2 == 0 for j in range(l + 1, 4))
            if l + 1 < 4:
                layer_casts(l + 1)
            if l % 2 == 0:
                i = l // 2
                qkv_tile(l, ctile, curc, later_attn)
                for t in tiles:
                    qkv_tile(l, t, cur, True)
                sch.barrier()
                attn_core(l, later_attn)
                sch.barrier()
            else:
                run_pass(sconv_tile, l, later_attn)
            run_pass(ffn_tile, l, later_attn)
            if l + 1 < nlayers:
                while ada_state.get(l + 1, 0) < 96:
                    ada_step(l + 1)
                ada_finish(l + 1)
        if nlayers == 0:
            for t in tiles:
                final_tile(t, cur)
        sch.barrier()
        MEMSET(rstd[:, 0:1], 0.0, [rstd_r])

        sch.finalize()
        with nc.Block() as block:
            @block.tensor
            def _(e):
                sch.replay("pe", e, esem, dsem)

            @block.scalar
            def _(e):
                sch.replay("act", e, esem, dsem)

            @block.vector
            def _(e):
                sch.replay("dve", e, esem, dsem)

            @block.sync
            def _(e):
                sch.replay("sp", e, esem, dsem)

            @block.gpsimd
            def _(e):
                sch.replay("pool", e, esem, dsem)
    return nc


def _cols(v):
    return np.ascontiguousarray(v.reshape(-1, 128).T)


def _slab(W, mc=128):
    K, M = W.shape
    return np.ascontiguousarray(W.reshape(K // 128, 128, M // mc, mc).transpose(2, 1, 0, 3)).reshape(M // mc, 128, (K // 128) * mc)


def _rope_tables(rot_dim, dup):
    rows = S // 64
    r = np.repeat(np.arange(rows, dtype=np.float32), 64)
    col = np.tile(np.arange(64, dtype=np.float32), rows)
    q = rot_dim // 4
    inv = (np.float32(10000.0) ** (-np.arange(q, dtype=np.float32) / np.float32(q))).astype(np.float32)
    ar = r[:, None] * inv
    ac = col[:, None] * inv
    ang = np.concatenate([ar, ar, ac, ac], axis=-1).astype(np.float32)
    cos = np.cos(ang).astype(np.float32).T
    sin = np.sin(ang).astype(np.float32).T
    sign = np.ones((rot_dim, 1), np.float32)
    sign[0:q] = -1.0
    sign[2 * q:3 * q] = -1.0
    sin = sin * sign
    if dup:
        cos = np.concatenate([cos, cos], 0)
        sin = np.concatenate([sin, sin], 0)
    return np.ascontiguousarray(np.stack([cos, sin], 0))


def _perm(rot_dim, reps):
    q = rot_dim // 4
    P = np.zeros((128, 128), np.float32)
    for rpt in range(reps):
        b = rpt * rot_dim
        for m in range(rot_dim):
            blk = m // q
            src = m + q if blk % 2 == 0 else m - q
            P[b + src, b + m] = 1.0
    return P


def prep_shared(inp):
    f = np.float32
    sh = {}
    wa = inp["w_ada"]
    sh["w_ada"] = np.concatenate([_slab(wa[l]) for l in range(4)], 0)
    b = np.stack([_cols(inp["b_ada"][l]) for l in range(4)], 1)
    sh["bada"] = np.ascontiguousarray(np.repeat(b[:, :, :, None], 2, axis=3)).reshape(128, 4 * 96 * 2)
    sh["nmix"] = np.concatenate([_cols(inp["norm_mix"][l]) for l in range(4)], 1)
    sh["nffn"] = np.concatenate([_cols(inp["norm_ffn"][l]) for l in range(4)], 1)
    sh["fnorm"] = _cols(inp["final_norm"])
    sh["fconv"] = np.concatenate([_cols(inp["ffn_conv"][l, j]) for l in range(4) for j in range(3)], 1)
    sh["sconv"] = np.concatenate([_cols(inp["sc_conv"][l, j]) for l in range(2) for j in range(3)], 1)
    hn = []
    for i in range(2):
        hn += [inp["attn_q_norm"][i][:, None], inp["attn_k_norm"][i][:, None], _cols(inp["mla_q_norm"][i]),
               _cols(inp["mla_kv_norm"][i])]
    sh["hnorm"] = np.ascontiguousarray(np.concatenate(hn, 1).astype(f))
    sh["ropeA"] = _rope_tables(128, False)
    sh["ropeB"] = _rope_tables(64, True)
    cs = np.zeros((128, 7, 128), f)
    cs[:, 0] = 1.0 / 2048
    cs[:, 1] = 1.0 / 128
    cs[:, 2] = 1.0 / 512
    cs[:, 3] = 1.0 / 256
    cs[:, 4] = 1.0
    cs[:, 5] = _perm(128, 1)
    cs[:, 6] = _perm(64, 2)
    sh["consts"] = cs.reshape(128, 7 * 128)
    w_in = []
    for i in range(2):
        W = inp["attn_w_in"][i]
        kr = W[:, 2304:2368]
        Wr = np.concatenate([W[:, 0:1024], W[:, 1024:1280], W[:, 1536:2048], W[:, 2048:2304], kr, kr, W[:, 1280:1536]], 1)
        w_in.append(_slab(Wr))
    sh["w_in"] = np.concatenate(w_in, 0)
    w_uq = []
    for i in range(2):
        W = inp["mla_w_uq"][i].reshape(512, 8, 192)
        Wr = np.concatenate([W[:, :, 0:128].reshape(512, 1024), W[:, :, 128:192].reshape(512, 512)], 1)
        w_uq.append(_slab(Wr))
    sh["w_uq"] = np.concatenate(w_uq, 0)
    w_ukv = []
    for i in range(2):
        W = inp["mla_w_ukv"][i].reshape(256, 8, 256)
        Wr = np.concatenate([W[:, :, 0:128].reshape(256, 1024), W[:, :, 128:256].reshape(256, 1024)], 1)
        w_ukv.append(_slab(Wr))
    sh["w_ukv"] = np.concatenate(w_ukv, 0)
    sh["w_o"] = np.concatenate([_slab(inp["attn_w_o"][i]) for i in range(2)], 0)
    sci = []
    for i in range(2):
        sl = _slab(inp["sc_w_in"][i]).reshape(3, 16, 128, 2048).transpose(1, 0, 2, 3).reshape(48, 128, 2048)
        sci.append(sl)
    sh["sc_in"] = np.ascontiguousarray(np.concatenate(sci, 0))
    sh["sc_out"] = np.concatenate([_slab(inp["sc_w_out"][i]) for i in range(2)], 0)
    fu = []
    for l in range(4):
        sl = _slab(inp["ffn_w_up"][l]).reshape(2, 44, 128, 2048).transpose(1, 0, 2, 3).reshape(88, 128, 2048)
        fu.append(sl)
    sh["f_up"] = np.ascontiguousarray(np.concatenate(fu, 0))
    sh["f_dn"] = np.concatenate([_slab(inp["ffn_w_down"][l]) for l in range(4)], 0)
    return sh


def run(inputs, nlayers=4, cores=8):
    inp = {k: np.asarray(v, dtype=np.float32) for k, v in inputs.items()}
    sh = prep_shared(inp)
    nc = build(nlayers)
    in_maps = []
    for b in range(cores):
        m = dict(sh)
        m["xT"] = np.ascontiguousarray(inp["x"][b].T)
        m["ctxT"] = np.ascontiguousarray(inp["ctx"][b].T)
        cv = np.stack([_cols(inp["c"][b]), _cols(inp["c_ctx"])], 2)
        m["cvec"] = np.ascontiguousarray(cv).reshape(128, 32)
        in_maps.append(m)
    res = run_bass_kernel_spmd(nc, in_maps, core_ids=list(range(cores)))
    out = np.stack([np.ascontiguousarray(r["outT"].T) for r in res.results], 0)
    return out.astype(np.float32)


def kernel(**inputs):
    return run(inputs, 4, 8)
```

```python
import numpy as np
from contextlib import ExitStack
import concourse.bass as bass
import concourse.mybir as mybir
from concourse.bass_utils import run_bass_kernel_spmd

F32 = mybir.dt.float32
BF16 = mybir.dt.bfloat16
AF = mybir.ActivationFunctionType
ALU = mybir.AluOpType

D = 2048
S = 4096
CT = 256
NKEY = S + CT
DFF = 5632
NF = DFF // 128
TV = 456
NTL = 9
NMAX = 458
EPS = 1e-6
A_SCALE = 128 ** -0.5
B_SCALE = 192 ** -0.5
NWB = 5
WBE = 6144
NDS = 40
NCS = 8
ENGS = ("pe", "act", "dve", "sp", "pool")


class Reg:
    __slots__ = ("w", "rd", "rdma")

    def __init__(self):
        self.w = None
        self.rd = {}
        self.rdma = []


class Op:
    __slots__ = ("eng", "fn", "deps", "dma", "dsem", "dcnt", "signal", "cnt")


class Sched:
    def __init__(self):
        self.ops = {e: [] for e in ENGS}
        self.nd = [0, 0]
        self.dma_last = [[None] * NDS, [None] * NCS]
        self.dma_cnt = [[0] * NDS, [0] * NCS]
        self.bar = {}

    def add(self, eng, fn, reads=(), writes=(), dma=0):
        o = Op()
        o.eng = eng
        o.fn = fn
        o.dma = dma
        o.signal = False
        o.cnt = 0
        deps = []
        for r in reads:
            if r.w is not None:
                deps.append(r.w)
        for w in writes:
            if w.w is not None:
                deps.append(w.w)
            deps.extend(w.rd.values())
            deps.extend(w.rdma)
        if eng in self.bar:
            deps.extend(self.bar.pop(eng))
        if dma:
            pool = dma - 1
            n = NDS if pool == 0 else NCS
            k = self.nd[pool] % n
            self.nd[pool] += 1
            if self.dma_last[pool][k] is not None:
                deps.append(self.dma_last[pool][k])
            self.dma_last[pool][k] = o
            self.dma_cnt[pool][k] += 16
            o.dsem = (pool, k)
            o.dcnt = self.dma_cnt[pool][k]
        seen = set()
        dl = []
        for d in deps:
            if id(d) in seen:
                continue
            seen.add(id(d))
            if (not d.dma) and (not dma) and d.eng == "pe" and eng == "pe":
                continue
            dl.append(d)
            if not d.dma:
                d.signal = True
        o.deps = dl
        for r in reads:
            if dma:
                r.rdma.append(o)
            else:
                r.rd[eng] = o
        for w in writes:
            w.w = o
            w.rd = {}
            w.rdma = []
        self.ops[eng].append(o)
        return o

    def barrier(self):
        deps = []
        for e in ENGS:
            for o in reversed(self.ops[e]):
                if not o.dma:
                    deps.append(o)
                    break
        deps.extend(o for o in self.dma_last[0] if o is not None)
        for e in ENGS:
            self.bar[e] = list(deps) + self.bar.get(e, [])

    def finalize(self):
        for e in ENGS:
            c = 0
            for o in self.ops[e]:
                if o.signal and not o.dma:
                    c += 1
                    o.cnt = c

    def replay(self, eng, h, esem, dsem):
        waited = {}
        for o in self.ops[eng]:
            for d in o.deps:
                if d.dma:
                    key = d.dsem
                    val = d.dcnt
                    sem = dsem[d.dsem[0]][d.dsem[1]]
                else:
                    key = d.eng
                    val = d.cnt
                    sem = esem[d.eng]
                if waited.get(key, 0) < val:
                    h.wait_ge(sem, val)
                    waited[key] = val
            ins = o.fn(h)
            if o.dma:
                ins.then_inc(dsem[o.dsem[0]][o.dsem[1]], 16)
            elif o.signal:
                ins.then_inc(esem[o.eng], 1)


def make_tiles():
    tiles = []
    for i in range(NTL):
        t0 = TV * i - 1
        hi = min(NMAX, S - t0)
        tiles.append(dict(kind=0, i=i, t0=t0, n=NMAX, lo=1 if i == 0 else 0, hi=hi, olo=1, ohi=min(NMAX - 1, hi)))
    ctx = dict(kind=1, i=0, t0=-1, n=CT + 2, lo=1, hi=CT + 1, olo=1, ohi=CT + 1)
    return tiles, ctx


def build(nlayers=4):
    nc = bass.Bass("TRN2", target_bir_lowering=False)
    sch = Sched()
    NOREG = ()

    def din(name, shape, dt=F32):
        return nc.dram_tensor(name, list(shape), dt, kind="ExternalInput").ap()

    def dscr(name, shape, dt=BF16):
        return nc.dram_tensor(name, list(shape), dt, kind="Internal").ap()

    xT = din("xT", [D, S])
    cT = din("ctxT", [D, CT])
    cvec = din("cvec", [128, 32])
    w_ada = din("w_ada", [4 * 96, 128, 2048])
    bada = din("bada", [128, 4 * 96 * 2])
    nmix_d = din("nmix", [128, 64])
    nffn_d = din("nffn", [128, 64])
    fnorm_d = din("fnorm", [128, 16])
    fconv_d = din("fconv", [128, 4 * 3 * 88])
    sconv_d = din("sconv", [128, 2 * 3 * 16])
    hn_d = din("hnorm", [128, 2 * 8])
    ropeA = din("ropeA", [2, 128, S])
    ropeB = din("ropeB", [2, 128, S])
    consts_d = din("consts", [128, 7 * 128])
    wsrc = {
        "w_in": din("w_in", [2 * 19, 128, 2048]),
        "w_uq": din("w_uq", [2 * 12, 128, 512]),
        "w_ukv": din("w_ukv", [2 * 16, 128, 256]),
        "w_o": din("w_o", [2 * 16, 128, 2048]),
        "sc_in": din("sc_in", [2 * 48, 128, 2048]),
        "sc_out": din("sc_out", [2 * 16, 128, 2048]),
        "f_up": din("f_up", [4 * 88, 128, 2048]),
        "f_dn": din("f_dn", [4 * 16, 128, 5632]),
    }
    wsrc["ada"] = w_ada
    outT = nc.dram_tensor("outT", [D, S], F32, kind="ExternalOutput").ap()

    wbf = {k: dscr(k + "_bf", v.shape) for k, v in wsrc.items()}
    wreg = {k: [Reg() for _ in range(v.shape[0])] for k, v in wsrc.items()}
    X = [dscr("X0", [D, S], F32), dscr("X1", [D, S], F32)]
    XC = [dscr("XC0", [D, CT], F32), dscr("XC1", [D, CT], F32)]
    QA = dscr("QA", [8 * 128, S])
    QBN = dscr("QBN", [8 * 128, S])
    QBR = dscr("QBR", [4 * 128, S])
    QAc = dscr("QAc", [8 * 128, CT])
    QBNc = dscr("QBNc", [8 * 128, CT])
    QBRc = dscr("QBRc", [4 * 128, CT])
    KA = dscr("KA", [2 * 128, NKEY])
    KBN = dscr("KBN", [8 * 128, NKEY])
    KR = dscr("KR", [128, NKEY])
    VA = dscr("VA", [NKEY, 256])
    VB = dscr("VB", [NKEY, 1024])
    OT = dscr("OT", [D, S])
    OTc = dscr("OTc", [D, CT])

    es = ExitStack()
    with es:
        def sb(name, shape, dt):
            return es.enter_context(nc.sbuf_tensor(name, list(shape), dt))

        xt = sb("xt", [128, 16, NMAX], F32)
        ht = sb("ht", [128, 16, NMAX], BF16)
        sq = sb("sq", [128, 4, 512], BF16)
        rstd = sb("rstd", [128, 512], F32)
        tmp = sb("tmp", [128, 8, 512], F32)
        big = sb("big", [128, 20736], BF16)
        wb = sb("wb", [128, NWB, WBE], BF16)
        ropA = sb("ropA", [128, 2, NMAX], F32)
        ropB = sb("ropB", [128, 2, NMAX], F32)
        cst_f = sb("cst_f", [128, 7 * 128], F32)
        cst = sb("cst", [128, 7, 128], BF16)
        cv_f = sb("cv_f", [128, 32], F32)
        cv_b = sb("cv_b", [128, 32], BF16)
        modv = sb("modv", [128, 4 * 96, 2], F32)
        bada_s = sb("bada_s", [128, 4 * 96 * 2], F32)
        nmix = sb("nmix_s", [128, 64], F32)
        nffn = sb("nffn_s", [128, 64], F32)
        fnorm = sb("fnorm_s", [128, 16], F32)
        fconv = sb("fconv_s", [128, 4 * 3 * 88], F32)
        sconv = sb("sconv_s", [128, 2 * 3 * 16], F32)
        hn = sb("hn_s", [128, 16], F32)
        hns = sb("hns_s", [128, 16], F32)
        AB = sb("AB", [128, 4 * 2 * 2, 16], F32)
        cstg = sb("cstg", [128, 2, 4096], BF16)
        ps = [es.enter_context(nc.psum_tensor("ps%d" % i, [128, 512], F32)) for i in range(8)]

        esem = {e: es.enter_context(nc.semaphore("se_" + e)) for e in ("pe", "act", "dve", "pool")}
        esem["sp"] = esem["pe"]
        dsem = [[es.enter_context(nc.semaphore("sd%d" % i)) for i in range(NDS)],
                [es.enter_context(nc.semaphore("sc%d" % i)) for i in range(NCS)]]

        xt_r = [Reg() for _ in range(16)]
        ht_r = [Reg() for _ in range(16)]
        sq_r = [Reg() for _ in range(4)]
        rstd_r = Reg()
        tmp_r = [Reg() for _ in range(8)]
        wb_r = [Reg() for _ in range(NWB)]
        ps_r = [Reg() for _ in range(8)]
        ropA_r = Reg()
        ropB_r = Reg()
        cst_r = Reg()
        vec_r = Reg()
        rot = {"sq": 0, "tmp": 0, "wb": 0, "ps": 0, "big": 0, "vst": 0, "cstg": 0}
        cstg_r = [Reg(), Reg()]

        def nxt(k, n):
            v = rot[k]
            rot[k] = (v + 1) % n
            return v

        def MM(out, lhsT, rhs, start, stop, reads, writes):
            sch.add("pe", lambda e: e.matmul(out, lhsT, rhs, start=start, stop=stop), reads, writes)

        def ACT(out, in_, func, reads, writes, bias=0.0, scale=1.0):
            sch.add("act", lambda e: e.activation(out=out, in_=in_, func=func, bias=bias, scale=scale), reads, writes)

        def TS(out, in0, s1, s2, op0, op1, reads, writes):
            if s2 is None:
                sch.add("dve", lambda e: e.tensor_scalar(out, in0, s1, None, op0), reads, writes)
            else:
                sch.add("dve", lambda e: e.tensor_scalar(out, in0, s1, s2, op0, op1), reads, writes)

        def STT(out, in0, scalar, in1, op0, op1, reads, writes):
            sch.add("dve", lambda e: e.scalar_tensor_tensor(out, in0, scalar, in1, op0, op1), reads, writes)

        def TT(out, in0, in1, op, reads, writes):
            sch.add("dve", lambda e: e.tensor_tensor(out, in0, in1, op), reads, writes)

        def RECIP(out, in_, reads, writes):
            sch.add("dve", lambda e: e.reciprocal(out, in_), reads, writes)

        def COPY(out, in_, reads, writes):
            sch.add("dve", lambda e: e.tensor_copy(out, in_), reads, writes)

        def MEMSET(ap, v, writes):
            sch.add("dve", lambda e: e.memset(ap, v), (), writes)

        def DMA(eng, out, in_, reads, writes, pool=1):
            sch.add(eng, lambda e: e.dma_start(out=out, in_=in_), reads, writes, dma=pool)

        cast_q = []

        def queue_cast(name, s0, s1, step):
            w = wsrc[name].shape[2]
            if w > 4096:
                h = w // 2
                for a in range(s0, s1):
                    cast_q.append((name, a, a + 1, 0, h))
                    cast_q.append((name, a, a + 1, h, w))
                return
            step = max(1, 4096 // w)
            for a in range(s0, s1, step):
                b = min(a + step, s1)
                cast_q.append((name, a, b, 0, w))

        def pump(nbytes):
            done = 0
            while cast_q and done < nbytes:
                name, a, b, c0, c1 = cast_q.pop(0)
                src = wsrc[name]
                ns, w = b - a, c1 - c0
                k = nxt("cstg", 2)
                stage = cstg[:, k, 0:ns * w].rearrange("p (s f) -> p s f", s=ns)
                DMA("pool", stage, src[a:b, :, c0:c1].rearrange("s p f -> p s f"), (), [cstg_r[k]], pool=2)
                DMA("pool", wbf[name][a:b, :, c0:c1].rearrange("s p f -> p s f"), stage, [cstg_r[k]], wreg[name][a:b], pool=2)
                done += ns * 128 * w * 4

        def layer_casts(l):
            i = l // 2
            if l >= 1:
                queue_cast("ada", l * 96, l * 96 + 96, 2)
            if l % 2 == 0:
                queue_cast("w_in", i * 19, i * 19 + 19, 2)
                queue_cast("w_uq", i * 12, i * 12 + 12, 12)
                queue_cast("w_ukv", i * 16, i * 16 + 16, 8)
                queue_cast("w_o", i * 16, i * 16 + 16, 2)
            else:
                queue_cast("sc_in", i * 48, i * 48 + 48, 2)
                queue_cast("sc_out", i * 16, i * 16 + 16, 2)
            queue_cast("f_up", l * 88, l * 88 + 88, 2)
            queue_cast("f_dn", l * 16, l * 16 + 16, 1)

        def load_w(name, s0, ns):
            i = nxt("wb", NWB)
            per = wbf[name].shape[2]
            dst = wb[:, i, 0:ns * per].rearrange("p (s f) -> p s f", s=ns)
            DMA("sp", dst, wbf[name][s0:s0 + ns, :, :].rearrange("s p f -> p s f"), wreg[name][s0:s0 + ns], [wb_r[i]])
            return i

        DMA("sp", cst_f[:, :], consts_d[:, :], (), [cst_r])
        COPY(cst[:, :, :], cst_f[:, :].rearrange("p (a b) -> p a b", a=7), [cst_r], [cst_r])
        for dst_, src_ in ((bada_s, bada), (nmix, nmix_d), (nffn, nffn_d), (fnorm, fnorm_d), (fconv, fconv_d),
                           (sconv, sconv_d), (cv_f, cvec), (hn, hn_d)):
            DMA("sp", dst_[:, :], src_[:, :], (), [vec_r])
        ACT(cv_b[:, :], cv_f[:, :], AF.Silu, [vec_r], [vec_r])
        TS(hns[:, :], hn[:, :], A_SCALE, None, ALU.mult, None, [vec_r], [vec_r])
        MEMSET(xt[:, :, :], 0.0, xt_r)
        MEMSET(ht[:, :, :], 0.0, ht_r)
        MEMSET(tmp[:, :, :], 0.0, tmp_r)
        MEMSET(sq[:, :, :], 0.0, sq_r)
        MEMSET(big[:, :], 0.0, ())
        MEMSET(rstd[:, :], 0.0, [rstd_r])
        sch.barrier()

        layer_casts(0)
        pump(30 << 20)

        ADA_PS = 7
        ada_state = {}

        def ada_slab(l, jj):
            if l == 0:
                i = nxt("wb", NWB)
                dst = wb[:, i, 0:2048]
                DMA("pool", dst, w_ada[l * 96 + jj, :, :], (), [wb_r[i]], pool=2)
            else:
                i = load_w("ada", l * 96 + jj, 1)
            for kc in range(16):
                MM(ps[ADA_PS][:, 2 * jj:2 * jj + 2], wb[:, i, kc * 128:kc * 128 + 128],
                   cv_b[:, 2 * kc:2 * kc + 2], kc == 0, kc == 15, [wb_r[i], vec_r], [ps_r[ADA_PS]])

        def ada_finish(l):
            TT(modv[:, l * 96:(l + 1) * 96, :].rearrange("p a b -> p (a b)"), ps[ADA_PS][:, 0:192],
               bada_s[:, l * 192:(l + 1) * 192], ALU.add, [ps_r[ADA_PS], vec_r], [vec_r])
            sch.barrier()
            for which in range(2):
                for s_ in range(2):
                    nw = (nmix if which == 0 else nffn)[:, l * 16:(l + 1) * 16]
                    sc0 = l * 96 + which * 48 + 16
                    STT(AB[:, (l * 2 + which) * 2 + s_, :], modv[:, sc0:sc0 + 16, s_], 1.0, nw, ALU.add, ALU.mult,
                        [vec_r], [vec_r])
            sch.barrier()

        def ada_step(l):
            jj = ada_state.get(l, 0)
            if l < nlayers and jj < 96:
                ada_slab(l, jj)
                ada_state[l] = jj + 1

        if nlayers > 0:
            for jj in range(96):
                ada_step(0)
            ada_finish(0)

        def modcol(l, j, c, s):
            return modv[:, l * 96 + j * 16 + c, s:s + 1]

        def Acol(l, which, c, s):
            return AB[:, (l * 2 + which) * 2 + s, c:c + 1]

        tiles, ctile = make_tiles()

        def load_x(t, src):
            a, b = t["lo"], t["hi"]
            DMA("sp", xt[:, :, a:b], src.rearrange("(c p) t -> p c t", p=128)[:, :, t["t0"] + a:t["t0"] + b], (), xt_r)

        def store_x(t, dst):
            a, b = t["olo"], t["ohi"]
            DMA("act", dst.rearrange("(c p) t -> p c t", p=128)[:, :, t["t0"] + a:t["t0"] + b], xt[:, :, a:b], xt_r, ())

        def stats(n, ones_idx, srcs):
            pb = nxt("ps", 7)
            for k, (ap, regs) in enumerate(srcs):
                j = nxt("sq", 4)
                ACT(sq[:, j, :n], ap, AF.Square, regs, [sq_r[j]])
                MM(ps[pb][:, :n], cst[:, ones_idx, :], sq[:, j, :n], k == 0, k == len(srcs) - 1,
                   [sq_r[j], cst_r], [ps_r[pb]])
            ACT(rstd[:, :n], ps[pb][:, :n], AF.Sqrt, [ps_r[pb]], [rstd_r], bias=EPS)
            RECIP(rstd[:, :n], rstd[:, :n], [rstd_r], [rstd_r])

        def norm_mod(t, l, which, zero_pad):
            n = t["n"]
            s = t["kind"]
            stats(n, 0, [(xt[:, c, :n], [xt_r[c]]) for c in range(16)])
            for c in range(16):
                k = nxt("tmp", 8)
                TT(tmp[:, k, :n], xt[:, c, :n], rstd[:, :n], ALU.mult, [xt_r[c], rstd_r], [tmp_r[k]])
                ACT(ht[:, c, :n], tmp[:, k, :n], AF.Identity, [tmp_r[k], vec_r], [ht_r[c]],
                    bias=modcol(l, which * 3, c, s), scale=Acol(l, which, c, s))
            if zero_pad:
                if t["lo"] > 0:
                    MEMSET(ht[:, :, 0:t["lo"]], 0.0, ht_r)
                if t["hi"] < n:
                    MEMSET(ht[:, :, t["hi"]:n], 0.0, ht_r)

        def resid(t, c, pb, n2, off, l, gj):
            s = t["kind"]
            STT(xt[:, c, off:off + n2], ps[pb][:, :n2], modcol(l, gj, c, s), xt[:, c, off:off + n2],
                ALU.mult, ALU.add, [ps_r[pb], xt_r[c], vec_r], [xt_r[c]])

        z_r = [Reg() for _ in range(16)]

        def z_ap(c, n2):
            return big[:, c * NMAX:c * NMAX + n2]

        def outproj(t, l, wname, s0, rhs_fn, rhs_regs, n2, off):
            for c in range(16):
                wi = load_w(wname, s0 + c, 1)
                pb = nxt("ps", 7)
                for kc in range(16):
                    MM(ps[pb][:, :n2], wb[:, wi, kc * 128:kc * 128 + 128], rhs_fn(kc), kc == 0, kc == 15,
                       [wb_r[wi], rhs_regs[kc]], [ps_r[pb]])
                resid(t, c, pb, n2, off, l, 2)

        def sconv_tile(l, t, src, dst):
            i = l // 2
            n = t["n"]
            n2 = n - 2
            load_x(t, src)
            norm_mod(t, l, 0, True)
            for c in range(16):
                pump(1 << 19)
                wi = load_w("sc_in", i * 48 + c * 3, 3)
                pbs = []
                for j in range(3):
                    pb = nxt("ps", 7)
                    pbs.append(pb)
                    for kc in range(16):
                        MM(ps[pb][:, :n], wb[:, wi, j * 2048 + kc * 128:j * 2048 + kc * 128 + 128], ht[:, kc, :n],
                           kc == 0, kc == 15, [wb_r[wi], ht_r[kc]], [ps_r[pb]])
                kv = nxt("tmp", 8)
                ACT(tmp[:, kv, :n], ps[pbs[2]][:, :n], AF.Identity, [ps_r[pbs[2]]], [tmp_r[kv]])
                TT(tmp[:, kv, :n], ps[pbs[1]][:, :n], tmp[:, kv, :n], ALU.mult, [ps_r[pbs[1]], tmp_r[kv]], [tmp_r[kv]])
                k = nxt("tmp", 8)
                w = [sconv[:, (i * 3 + tap) * 16 + c:(i * 3 + tap) * 16 + c + 1] for tap in range(3)]
                TS(tmp[:, k, :n2], tmp[:, kv, 0:n2], w[0], None, ALU.mult, None, [tmp_r[kv], vec_r], [tmp_r[k]])
                STT(tmp[:, k, :n2], tmp[:, kv, 1:n2 + 1], w[1], tmp[:, k, :n2], ALU.mult, ALU.add,
                    [tmp_r[kv], vec_r, tmp_r[k]], [tmp_r[k]])
                STT(tmp[:, k, :n2], tmp[:, kv, 2:n2 + 2], w[2], tmp[:, k, :n2], ALU.mult, ALU.add,
                    [tmp_r[kv], vec_r, tmp_r[k]], [tmp_r[k]])
                TT(z_ap(c, n2), ps[pbs[0]][:, 1:n2 + 1], tmp[:, k, :n2], ALU.mult, [ps_r[pbs[0]], tmp_r[k]], [z_r[c]])
            outproj(t, l, "sc_out", i * 16, lambda kc: z_ap(kc, n2), z_r, n2, 1)
            store_x(t, dst)

        act_r = [Reg() for _ in range(NF)]

        def act_ap(f, n2):
            return big[:, f * NMAX:f * NMAX + n2]

        def ffn_tile(l, t, src, dst):
            n = t["n"]
            n2 = n - 2
            load_x(t, src)
            if l % 2 == 0:
                a, b = t["lo"], t["hi"]
                osrc = OT if t["kind"] == 0 else OTc
                otv = big[:, 0:16 * NMAX].rearrange("p (c n) -> p c n", c=16)
                DMA("sp", otv[:, :, a:b], osrc.rearrange("(c p) t -> p c t", p=128)[:, :, t["t0"] + a:t["t0"] + b],
                    (), act_r[0:16])
                outproj(t, l, "w_o", (l // 2) * 16, lambda kc: big[:, kc * NMAX:kc * NMAX + n], act_r, n, 0)
            norm_mod(t, l, 1, True)
            for f in range(NF):
                if f % 4 == 3:
                    ada_step(l + 1)
                if f % 4 == 1:
                    pump(2 << 20)
                wi = load_w("f_up", l * 88 + 2 * f, 2)
                tk = []
                for j in range(2):
                    pb = nxt("ps", 7)
                    for kc in range(16):
                        MM(ps[pb][:, :n], wb[:, wi, j * 2048 + kc * 128:j * 2048 + kc * 128 + 128], ht[:, kc, :n],
                           kc == 0, kc == 15, [wb_r[wi], ht_r[kc]], [ps_r[pb]])
                    k = nxt("tmp", 8)
                    ch = j * NF + f
                    w = [fconv[:, (l * 3 + tap) * 88 + ch:(l * 3 + tap) * 88 + ch + 1] for tap in range(3)]
                    TS(tmp[:, k, :n2], ps[pb][:, 0:n2], w[0], None, ALU.mult, None, [ps_r[pb], vec_r], [tmp_r[k]])
                    STT(tmp[:, k, :n2], ps[pb][:, 1:n2 + 1], w[1], tmp[:, k, :n2], ALU.mult, ALU.add,
                        [ps_r[pb], vec_r, tmp_r[k]], [tmp_r[k]])
                    STT(tmp[:, k, :n2], ps[pb][:, 2:n2 + 2], w[2], tmp[:, k, :n2], ALU.mult, ALU.add,
                        [ps_r[pb], vec_r, tmp_r[k]], [tmp_r[k]])
                    if j == 0:
                        ACT(tmp[:, k, :n2], tmp[:, k, :n2], AF.Silu, [tmp_r[k]], [tmp_r[k]])
                    tk.append(k)
                TT(act_ap(f, n2), tmp[:, tk[0], :n2], tmp[:, tk[1], :n2], ALU.mult,
                   [tmp_r[tk[0]], tmp_r[tk[1]]], [act_r[f]])
            for c in range(16):
                wi = load_w("f_dn", l * 16 + c, 1)
                pb = nxt("ps", 7)
                for kc in range(NF):
                    MM(ps[pb][:, :n2], wb[:, wi, kc * 128:kc * 128 + 128], act_ap(kc, n2), kc == 0, kc == NF - 1,
                       [wb_r[wi], act_r[kc]], [ps_r[pb]])
                resid(t, c, pb, n2, 1, l, 5)
            if l == nlayers - 1:
                stats(n, 0, [(xt[:, c, :n], [xt_r[c]]) for c in range(16)])
                for c in range(16):
                    STT(xt[:, c, :n], xt[:, c, :n], fnorm[:, c:c + 1], rstd[:, :n], ALU.mult, ALU.mult,
                        [xt_r[c], rstd_r, vec_r], [xt_r[c]])
                store_x(t, outT)
            else:
                store_x(t, dst)

        def attnout_tile(l, t, src, dst):
            i = l // 2
            n = t["n"]
            a, b = t["lo"], t["hi"]
            pump(14 << 20)
            load_x(t, src)
            osrc = OT if t["kind"] == 0 else OTc
            otv = big[:, 0:16 * NMAX].rearrange("p (c n) -> p c n", c=16)
            DMA("sp", otv[:, :, a:b], osrc.rearrange("(c p) t -> p c t", p=128)[:, :, t["t0"] + a:t["t0"] + b], (), z_r)
            outproj(t, l, "w_o", i * 16, lambda kc: big[:, kc * NMAX:kc * NMAX + n], z_r, n, 0)
            store_x(t, dst)

        stg_r = [Reg() for _ in range(8)]
        BIGQ = 8 * 512

        def stg(k, n):
            return big[:, k * 512:k * 512 + n]

        cqn_r = [Reg() for _ in range(6)]

        def cqn(k, n):
            return big[:, BIGQ + k * 512:BIGQ + k * 512 + n]

        vst_r = [Reg() for _ in range(2)]

        def vst(k):
            return big[:, BIGQ + 6 * 512 + k * 1024:BIGQ + 6 * 512 + (k + 1) * 1024]

        def store_feat(t, dst_lat, dst_ctx, row0, k, lat_off):
            a, b = t["olo"], t["ohi"]
            if t["kind"] == 0:
                d = dst_lat[row0:row0 + 128, lat_off + t["t0"] + a:lat_off + t["t0"] + b]
            else:
                d = dst_ctx[row0:row0 + 128, t["t0"] + a:t["t0"] + b]
            DMA("pool", d, stg(k, NMAX)[:, a:b], [stg_r[k]], ())

        def rope_tail(n, src_k, sq_j, perm_idx, rop, rop_r, outk):
            pb = nxt("ps", 7)
            MM(ps[pb][:, :n], cst[:, perm_idx, :], sq[:, sq_j, :n], True, True, [sq_r[sq_j], cst_r], [ps_r[pb]])
            k1 = nxt("tmp", 8)
            TT(tmp[:, k1, :n], tmp[:, src_k, :n], rop[:, 0, :n], ALU.mult, [tmp_r[src_k], rop_r], [tmp_r[k1]])
            k2 = nxt("tmp", 8)
            TT(tmp[:, k2, :n], ps[pb][:, :n], rop[:, 1, :n], ALU.mult, [ps_r[pb], rop_r], [tmp_r[k2]])
            TT(stg(outk, n), tmp[:, k1, :n], tmp[:, k2, :n], ALU.add, [tmp_r[k1], tmp_r[k2]], [stg_r[outk]])

        def proj_fm(wi, woff, KC, rhs_fn, rhs_regs, n):
            pb = nxt("ps", 7)
            for kc in range(KC):
                MM(ps[pb][:, :n], wb[:, wi, woff + kc * 128:woff + kc * 128 + 128], rhs_fn(kc), kc == 0, kc == KC - 1,
                   [wb_r[wi], rhs_regs[kc]], [ps_r[pb]])
            return pb

        def pipeline(tasks):
            nt = len(tasks)
            for step in range(nt + 2):
                for st_ in range(3):
                    k = step - st_
                    if 0 <= k < nt and st_ < len(tasks[k]):
                        tasks[k][st_]()

        def qkv_tile(l, t, src, with_q):
            i = l // 2
            n = t["n"]
            lat = t["kind"] == 0
            koff = CT if lat else 0
            load_x(t, src)
            norm_mod(t, l, 0, False)
            if lat:
                a, b = t["lo"], t["hi"]
                DMA("sp", ropA[:, :, a:b], ropeA.rearrange("k p t -> p k t")[:, :, t["t0"] + a:t["t0"] + b], (), [ropA_r])
                DMA("sp", ropB[:, :, a:b], ropeB.rearrange("k p t -> p k t")[:, :, t["t0"] + a:t["t0"] + b], (), [ropB_r])
            hfn = lambda kc: ht[:, kc, :n]
            hb = i * 8

            def head_task(slab, gain_ap, dst_lat, dst_ctx, row0, lat_off):
                st = {}

                def s1():
                    wi = load_w("w_in", i * 19 + slab, 1)
                    st["pb"] = proj_fm(wi, 0, 16, hfn, ht_r, n)

                def s2():
                    pb = st["pb"]
                    stats(n, 1, [(ps[pb][:, :n], [ps_r[pb]])])
                    k = nxt("tmp", 8)
                    st["k"] = k
                    TT(tmp[:, k, :n], ps[pb][:, :n], rstd[:, :n], ALU.mult, [ps_r[pb], rstd_r], [tmp_r[k]])
                    if not lat:
                        ko = nxt("big", 8)
                        st["ko"] = ko
                        ACT(stg(ko, n), tmp[:, k, :n], AF.Identity, [tmp_r[k], vec_r], [stg_r[ko]], scale=gain_ap)
                    else:
                        ACT(tmp[:, k, :n], tmp[:, k, :n], AF.Identity, [tmp_r[k], vec_r], [tmp_r[k]], scale=gain_ap)
                        j = nxt("sq", 4)
                        st["j"] = j
                        ACT(sq[:, j, :n], tmp[:, k, :n], AF.Identity, [tmp_r[k]], [sq_r[j]])

                def s3():
                    if lat:
                        ko = nxt("big", 8)
                        st["ko"] = ko
                        rope_tail(n, st["k"], st["j"], 5, ropA, ropA_r, ko)
                    store_feat(t, dst_lat, dst_ctx, row0, st["ko"], lat_off)

                return [s1, s2, s3]

            def latent_tasks(c0, ncn, ones_idx, g0, k0):
                sh = {"ks": []}
                tasks = []
                for c in range(ncn):
                    def mk(c):
                        st = {}

                        def s1():
                            wi = load_w("w_in", i * 19 + c0 + c, 1)
                            st["pb"] = proj_fm(wi, 0, 16, hfn, ht_r, n)

                        def s2():
                            pb = st["pb"]
                            if c == 0:
                                sh["pbs"] = nxt("ps", 7)
                            pbs = sh["pbs"]
                            k = nxt("tmp", 8)
                            sh["ks"].append(k)
                            ACT(tmp[:, k, :n], ps[pb][:, :n], AF.Identity, [ps_r[pb]], [tmp_r[k]])
                            j = nxt("sq", 4)
                            ACT(sq[:, j, :n], ps[pb][:, :n], AF.Square, [ps_r[pb]], [sq_r[j]])
                            MM(ps[pbs][:, :n], cst[:, ones_idx, :], sq[:, j, :n], c == 0, c == ncn - 1,
                               [sq_r[j], cst_r], [ps_r[pbs]])
                            if c == ncn - 1:
                                ACT(rstd[:, :n], ps[pbs][:, :n], AF.Sqrt, [ps_r[pbs]], [rstd_r], bias=EPS)
                                RECIP(rstd[:, :n], rstd[:, :n], [rstd_r], [rstd_r])
                                for cc in range(ncn):
                                    kk = sh["ks"][cc]
                                    TT(tmp[:, kk, :n], tmp[:, kk, :n], rstd[:, :n], ALU.mult, [tmp_r[kk], rstd_r], [tmp_r[kk]])
                                    ACT(cqn(k0 + cc, n), tmp[:, kk, :n], AF.Identity, [tmp_r[kk], vec_r], [cqn_r[k0 + cc]],
                                        scale=hn[:, hb + g0 + cc:hb + g0 + cc + 1])

                        return [s1, s2]
                    tasks.append(mk(c))
                return tasks

            def simple_task(proj, scale, do_rope, rop, rop_r, dst_lat, dst_ctx, row0, lat_off):
                st = {}

                def s1():
                    st["pb"] = proj()

                def s2():
                    pb = st["pb"]
                    if do_rope:
                        kt = nxt("tmp", 8)
                        st["k"] = kt
                        ACT(tmp[:, kt, :n], ps[pb][:, :n], AF.Identity, [ps_r[pb]], [tmp_r[kt]], scale=scale)
                        j = nxt("sq", 4)
                        st["j"] = j
                        ACT(sq[:, j, :n], ps[pb][:, :n], AF.Identity, [ps_r[pb]], [sq_r[j]], scale=scale)
                    else:
                        ko = nxt("big", 8)
                        st["ko"] = ko
                        ACT(stg(ko, n), ps[pb][:, :n], AF.Identity, [ps_r[pb]], [stg_r[ko]], scale=scale)

                def s3():
                    if do_rope:
                        ko = nxt("big", 8)
                        st["ko"] = ko
                        rope_tail(n, st["k"], st["j"], 6, rop, rop_r, ko)
                    store_feat(t, dst_lat, dst_ctx, row0, st["ko"], lat_off)

                return [s1, s2, s3]

            g1 = []
            if with_q:
                for h in range(8):
                    g1.append(head_task(h, hns[:, hb:hb + 1], QA, QAc, h * 128, 0))
            for h in range(2):
                g1.append(head_task(8 + h, hn[:, hb + 1:hb + 2], KA, KA, h * 128, koff))
            if with_q:
                g1 += latent_tasks(10, 4, 2, 2, 0)
            g1 += latent_tasks(14, 2, 3, 6, 4)

            def kr_proj():
                wi = load_w("w_in", i * 19 + 16, 1)
                return proj_fm(wi, 0, 16, hfn, ht_r, n)
            g1.append(simple_task(kr_proj, 1.0, lat, ropB, ropB_r, KR, KR, 0, koff))

            blocks = []
            c0 = t["olo"]
            while c0 < t["ohi"]:
                m = min(128, t["ohi"] - c0)
                blocks.append((c0, m))
                c0 += m
            wv = {}

            def va_task(c0, m, first):
                st = {}
                row0 = koff + t["t0"] + c0

                def s1():
                    if first:
                        wv["a"] = load_w("w_in", i * 19 + 17, 2)
                    wva = wv["a"]
                    pb = nxt("ps", 7)
                    st["pb"] = pb
                    for j in range(2):
                        for kc in range(16):
                            MM(ps[pb][0:m, j * 128:j * 128 + 128], ht[:, kc, c0:c0 + m],
                               wb[:, wva, j * 2048 + kc * 128:j * 2048 + kc * 128 + 128], kc == 0, kc == 15,
                               [wb_r[wva], ht_r[kc]], [ps_r[pb]])

                def s2():
                    pb = st["pb"]
                    vk = nxt("vst", 2)
                    ACT(vst(vk)[0:m, 0:256], ps[pb][0:m, 0:256], AF.Identity, [ps_r[pb]], [vst_r[vk]])
                    DMA("pool", VA[row0:row0 + m, :], vst(vk)[0:m, 0:256], [vst_r[vk]], ())

                return [s1, s2]

            for bi, (c0, m) in enumerate(blocks):
                g1.append(va_task(c0, m, bi == 0))
            pipeline(g1)

            g2 = []
            if with_q:
                def mkq(off):
                    def f():
                        if "q" not in wv:
                            wv["q"] = load_w("w_uq", i * 12, 12)
                        return proj_fm(wv["q"], off, 4, lambda kc: cqn(kc, n), cqn_r[0:4], n)
                    return f
                for h in range(8):
                    g2.append(simple_task(mkq(h * 512), B_SCALE, False, None, None, QBN, QBNc, h * 128, 0))
                for hp in range(4):
                    g2.append(simple_task(mkq((8 + hp) * 512), B_SCALE, lat, ropB, ropB_r, QBR, QBRc, hp * 128, 0))

            def mkk(h):
                def f():
                    if "k" not in wv:
                        wv["k"] = load_w("w_ukv", i * 16, 8)
                    return proj_fm(wv["k"], h * 256, 2, lambda kc: cqn(4 + kc, n), cqn_r[4:6], n)
                return f
            for h in range(8):
                g2.append(simple_task(mkk(h), 1.0, False, None, None, KBN, KBN, h * 128, koff))

            def vb_task(c0, m):
                st = {}
                row0 = koff + t["t0"] + c0

                def s1():
                    if "v" not in wv:
                        wv["v"] = load_w("w_ukv", i * 16 + 8, 8)
                    wvb = wv["v"]
                    st["pbs"] = []
                    for half in range(2):
                        pb = nxt("ps", 7)
                        st["pbs"].append(pb)
                        for j in range(4):
                            h = half * 4 + j
                            for kc in range(2):
                                MM(ps[pb][0:m, j * 128:j * 128 + 128], cqn(4 + kc, n)[:, c0:c0 + m],
                                   wb[:, wvb, h * 256 + kc * 128:h * 256 + kc * 128 + 128], kc == 0, kc == 1,
                                   [wb_r[wvb], cqn_r[4 + kc]], [ps_r[pb]])

                def s2():
                    vk = nxt("vst", 2)
                    for half in range(2):
                        pb = st["pbs"][half]
                        ACT(vst(vk)[0:m, half * 512:half * 512 + 512], ps[pb][0:m, 0:512], AF.Identity,
                            [ps_r[pb]], [vst_r[vk]])
                    DMA("pool", VB[row0:row0 + m, :], vst(vk)[0:m, :], [vst_r[vk]], ())

                return [s1, s2]

            for (c0, m) in blocks:
                g2.append(vb_task(c0, m))
            pipeline(g2)

        K_r, KR_r, V_r = Reg(), Reg(), Reg()
        Kb = big[:, 0:NKEY]
        KRb = big[:, NKEY:2 * NKEY]
        Vb = big[:, 2 * NKEY:3 * NKEY].rearrange("p (c d) -> p c d", c=34)
        QO = 3 * NKEY
        q_r = [Reg() for _ in range(2)]
        qr_r = [Reg() for _ in range(2)]
        p_r = [Reg() for _ in range(4)]
        o_r = [Reg() for _ in range(4)]

        def qb(k):
            return big[:, QO + k * 512:QO + (k + 1) * 512]

        def qrb(k):
            return big[:, QO + 1024 + k * 512:QO + 1024 + (k + 1) * 512]

        def pbuf(k):
            return big[:, QO + 2048 + k * 512:QO + 2048 + (k + 1) * 512]

        def obuf(k):
            return big[:, QO + 4096 + k * 512:QO + 4096 + (k + 1) * 512]

        att = {"q": 0, "p": 0, "o": 0, "s": 0, "acc": 0}

        dacc_r = [Reg() for _ in range(4)]
        dsum_r = [Reg() for _ in range(2)]

        def PADD(eng, out, in0, in1, reads, writes):
            sch.add(eng, lambda e: e.tensor_tensor(out, in0, in1, ALU.add), reads, writes)

        def PCOPY(eng, out, in_, reads, writes):
            sch.add(eng, lambda e: e.tensor_copy(out, in_), reads, writes)

        def q_load(blk):
            qi = att["q"]
            att["q"] ^= 1
            blk["qi"] = qi
            nq = blk["nq"]
            DMA("sp", qb(qi)[:, :nq], blk["qsrc"], (), [q_r[qi]])
            if blk["qrsrc"] is not None:
                DMA("sp", qrb(qi)[:, :nq], blk["qrsrc"], (), [qr_r[qi]])

        def attn_block(blk, nxt_blk):
            qsrc, qrsrc, half, nq, kcs, odst = (blk["qsrc"], blk["qrsrc"], blk["half"], blk["nq"], blk["kcs"], blk["odst"])
            if "qi" not in blk:
                q_load(blk)
            qi = blk["qi"]
            ab = att["acc"]
            att["acc"] ^= 1
            po, pd = 4 + ab, 6 + ab
            prev = None

            def pv(kc, pi, first, last):
                MM(ps[po][:, :nq], Vb[:, kc, :], pbuf(pi)[:, :nq], first, last, [V_r, p_r[pi]], [ps_r[po]])
                MM(ps[pd][:, :nq], cst[:, 4, :], pbuf(pi)[:, :nq], first, last, [cst_r, p_r[pi]], [ps_r[pd]])

            for idx, kc in enumerate(kcs):
                sbk = att["s"]
                att["s"] = (sbk + 1) % 4
                MM(ps[sbk][:, :nq], Kb[:, kc * 128:kc * 128 + 128], qb(qi)[:, :nq], True, qrsrc is None,
                   [K_r, q_r[qi]], [ps_r[sbk]])
                if qrsrc is not None:
                    MM(ps[sbk][:, :nq], KRb[half * 64:half * 64 + 64, kc * 128:kc * 128 + 128],
                       qrb(qi)[half * 64:half * 64 + 64, :nq], False, True, [KR_r, qr_r[qi]], [ps_r[sbk]])
                pi = att["p"]
                att["p"] = (pi + 1) % 4
                ACT(pbuf(pi)[:, :nq], ps[sbk][:, :nq], AF.Exp, [ps_r[sbk]], [p_r[pi]])
                if prev is not None:
                    pv(prev[0], prev[1], prev[2], False)
                prev = (kc, pi, idx == 0)
                if idx == 1 and nxt_blk is not None:
                    q_load(nxt_blk)
            pv(prev[0], prev[1], prev[2], True)
            k = nxt("tmp", 8)
            RECIP(tmp[:, k, :nq], ps[pd][:, :nq], [ps_r[pd]], [tmp_r[k]])
            oi = att["o"]
            att["o"] = (oi + 1) % 4
            TT(obuf(oi)[:, :nq], ps[po][:, :nq], tmp[:, k, :nq], ALU.mult, [ps_r[po], tmp_r[k]], [o_r[oi]])
            DMA("sp", odst, obuf(oi)[:, :nq], [o_r[oi]], ())

        def attn_core(l, ctx_q):
            allk = list(range(34))
            blocks = []

            def qtiles(Qd, Qdc, row0, Qr=None, Qrc=None, rrow0=0, half=0, ochunk=0, kvload=None):
                first = True
                for qt in range(8):
                    blocks.append(dict(qsrc=Qd[row0:row0 + 128, qt * 512:(qt + 1) * 512],
                                       qrsrc=None if Qr is None else Qr[rrow0:rrow0 + 128, qt * 512:(qt + 1) * 512],
                                       half=half, nq=512, kcs=allk,
                                       odst=OT[ochunk * 128:ochunk * 128 + 128, qt * 512:(qt + 1) * 512],
                                       newkv=kvload if first else None))
                    first = False
                if ctx_q:
                    blocks.append(dict(qsrc=Qdc[row0:row0 + 128, :], qrsrc=None if Qr is None else Qrc[rrow0:rrow0 + 128, :],
                                       half=half, nq=CT, kcs=[0, 1], odst=OTc[ochunk * 128:ochunk * 128 + 128, :], newkv=None))

            def kv_a(kv):
                def f():
                    DMA("sp", Kb, KA[kv * 128:kv * 128 + 128, :], (), [K_r])
                    DMA("sp", Vb, VA[:, kv * 128:kv * 128 + 128].rearrange("(c p) d -> p c d", p=128), (), [V_r])
                return f

            def kv_b(h):
                def f():
                    if h == 0:
                        DMA("sp", KRb, KR[:, :], (), [KR_r])
                    DMA("sp", Kb, KBN[h * 128:h * 128 + 128, :], (), [K_r])
                    DMA("sp", Vb, VB[:, h * 128:h * 128 + 128].rearrange("(c p) d -> p c d", p=128), (), [V_r])
                return f

            for kv in range(2):
                for g in range(4):
                    h = kv * 4 + g
                    qtiles(QA, QAc, h * 128, ochunk=h, kvload=kv_a(kv) if g == 0 else None)
            for h in range(8):
                qtiles(QBN, QBNc, h * 128, QBR, QBRc, (h // 2) * 128, h % 2, 8 + h, kvload=kv_b(h))
            for bi, blk in enumerate(blocks):
                pump(2 << 20)
                if blk["newkv"] is not None:
                    blk["newkv"]()
                attn_block(blk, blocks[bi + 1] if bi + 1 < len(blocks) else None)

        def final_tile(t, src):
            n = t["n"]
            load_x(t, src)
            stats(n, 0, [(xt[:, c, :n], [xt_r[c]]) for c in range(16)])
            for c in range(16):
                STT(xt[:, c, :n], xt[:, c, :n], fnorm[:, c:c + 1], rstd[:, :n], ALU.mult, ALU.mult,
                    [xt_r[c], rstd_r, vec_r], [xt_r[c]])
            store_x(t, outT)

        cur, curc = xT, cT
        nb, nbc = 0, 0

        def run_pass(fn, l, with_ctx, advance=True):
            nonlocal cur, curc, nb, nbc
            if with_ctx:
                fn(l, ctile, curc, XC[nbc])
            for t in tiles:
                fn(l, t, cur, X[nb])
            sch.barrier()
            if with_ctx:
                curc = XC[nbc]
                nbc ^= 1
            cur = X[nb]
            nb ^= 1

        for l in range(nlayers):
            later_attn = any(j % 2 == 0 for j in range(l + 1, 4))
            if l + 1 < 4:
                layer_casts(l + 1)
            if l % 2 == 0:
                i = l // 2
                qkv_tile(l, ctile, curc, later_attn)
                for t in tiles:
                    qkv_tile(l, t, cur, True)
                sch.barrier()
                attn_core(l, later_attn)
                sch.barrier()
            else:
                run_pass(sconv_tile, l, later_attn)
            run_pass(ffn_tile, l, later_attn)
            if l + 1 < nlayers:
                while ada_state.get(l + 1, 0) < 96:
                    ada_step(l + 1)
                ada_finish(l + 1)
        if nlayers == 0:
            for t in tiles:
                final_tile(t, cur)
        sch.barrier()
        MEMSET(rstd[:, 0:1], 0.0, [rstd_r])

        sch.finalize()
        with nc.Block() as block:
            @block.tensor
            def _(e):
                sch.replay("pe", e, esem, dsem)

            @block.scalar
            def _(e):
                sch.replay("act", e, esem, dsem)

            @block.vector
            def _(e):
                sch.replay("dve", e, esem, dsem)

            @block.sync
            def _(e):
                sch.replay("sp", e, esem, dsem)

            @block.gpsimd
            def _(e):
                sch.replay("pool", e, esem, dsem)
    return nc


def _cols(v):
    return np.ascontiguousarray(v.reshape(-1, 128).T)


def _slab(W, mc=128):
    K, M = W.shape
    return np.ascontiguousarray(W.reshape(K // 128, 128, M // mc, mc).transpose(2, 1, 0, 3)).reshape(M // mc, 128, (K // 128) * mc)


def _rope_tables(rot_dim, dup):
    rows = S // 64
    r = np.repeat(np.arange(rows, dtype=np.float32), 64)
    col = np.tile(np.arange(64, dtype=np.float32), rows)
    q = rot_dim // 4
    inv = (np.float32(10000.0) ** (-np.arange(q, dtype=np.float32) / np.float32(q))).astype(np.float32)
    ar = r[:, None] * inv
    ac = col[:, None] * inv
    ang = np.concatenate([ar, ar, ac, ac], axis=-1).astype(np.float32)
    cos = np.cos(ang).astype(np.float32).T
    sin = np.sin(ang).astype(np.float32).T
    sign = np.ones((rot_dim, 1), np.float32)
    sign[0:q] = -1.0
    sign[2 * q:3 * q] = -1.0
    sin = sin * sign
    if dup:
        cos = np.concatenate([cos, cos], 0)
        sin = np.concatenate([sin, sin], 0)
    return np.ascontiguousarray(np.stack([cos, sin], 0))


def _perm(rot_dim, reps):
    q = rot_dim // 4
    P = np.zeros((128, 128), np.float32)
    for rpt in range(reps):
        b = rpt * rot_dim
        for m in range(rot_dim):
            blk = m // q
            src = m + q if blk % 2 == 0 else m - q
            P[b + src, b + m] = 1.0
    return P


def prep_shared(inp):
    f = np.float32
    sh = {}
    wa = inp["w_ada"]
    sh["w_ada"] = np.concatenate([_slab(wa[l]) for l in range(4)], 0)
    b = np.stack([_cols(inp["b_ada"][l]) for l in range(4)], 1)
    sh["bada"] = np.ascontiguousarray(np.repeat(b[:, :, :, None], 2, axis=3)).reshape(128, 4 * 96 * 2)
    sh["nmix"] = np.concatenate([_cols(inp["norm_mix"][l]) for l in range(4)], 1)
    sh["nffn"] = np.concatenate([_cols(inp["norm_ffn"][l]) for l in range(4)], 1)
    sh["fnorm"] = _cols(inp["final_norm"])
    sh["fconv"] = np.concatenate([_cols(inp["ffn_conv"][l, j]) for l in range(4) for j in range(3)], 1)
    sh["sconv"] = np.concatenate([_cols(inp["sc_conv"][l, j]) for l in range(2) for j in range(3)], 1)
    hn = []
    for i in range(2):
        hn += [inp["attn_q_norm"][i][:, None], inp["attn_k_norm"][i][:, None], _cols(inp["mla_q_norm"][i]),
               _cols(inp["mla_kv_norm"][i])]
    sh["hnorm"] = np.ascontiguousarray(np.concatenate(hn, 1).astype(f))
    sh["ropeA"] = _rope_tables(128, False)
    sh["ropeB"] = _rope_tables(64, True)
    cs = np.zeros((128, 7, 128), f)
    cs[:, 0] = 1.0 / 2048
    cs[:, 1] = 1.0 / 128
    cs[:, 2] = 1.0 / 512
    cs[:, 3] = 1.0 / 256
    cs[:, 4] = 1.0
    cs[:, 5] = _perm(128, 1)
    cs[:, 6] = _perm(64, 2)
    sh["consts"] = cs.reshape(128, 7 * 128)
    w_in = []
    for i in range(2):
        W = inp["attn_w_in"][i]
        kr = W[:, 2304:2368]
        Wr = np.concatenate([W[:, 0:1024], W[:, 1024:1280], W[:, 1536:2048], W[:, 2048:2304], kr, kr, W[:, 1280:1536]], 1)
        w_in.append(_slab(Wr))
    sh["w_in"] = np.concatenate(w_in, 0)
    w_uq = []
    for i in range(2):
        W = inp["mla_w_uq"][i].reshape(512, 8, 192)
        Wr = np.concatenate([W[:, :, 0:128].reshape(512, 1024), W[:, :, 128:192].reshape(512, 512)], 1)
        w_uq.append(_slab(Wr))
    sh["w_uq"] = np.concatenate(w_uq, 0)
    w_ukv = []
    for i in range(2):
        W = inp["mla_w_ukv"][i].reshape(256, 8, 256)
        Wr = np.concatenate([W[:, :, 0:128].reshape(256, 1024), W[:, :, 128:256].reshape(256, 1024)], 1)
        w_ukv.append(_slab(Wr))
    sh["w_ukv"] = np.concatenate(w_ukv, 0)
    sh["w_o"] = np.concatenate([_slab(inp["attn_w_o"][i]) for i in range(2)], 0)
    sci = []
    for i in range(2):
        sl = _slab(inp["sc_w_in"][i]).reshape(3, 16, 128, 2048).transpose(1, 0, 2, 3).reshape(48, 128, 2048)
        sci.append(sl)
    sh["sc_in"] = np.ascontiguousarray(np.concatenate(sci, 0))
    sh["sc_out"] = np.concatenate([_slab(inp["sc_w_out"][i]) for i in range(2)], 0)
    fu = []
    for l in range(4):
        sl = _slab(inp["ffn_w_up"][l]).reshape(2, 44, 128, 2048).transpose(1, 0, 2, 3).reshape(88, 128, 2048)
        fu.append(sl)
    sh["f_up"] = np.ascontiguousarray(np.concatenate(fu, 0))
    sh["f_dn"] = np.concatenate([_slab(inp["ffn_w_down"][l]) for l in range(4)], 0)
    return sh


def run(inputs, nlayers=4, cores=8):
    inp = {k: np.asarray(v, dtype=np.float32) for k, v in inputs.items()}
    sh = prep_shared(inp)
    nc = build(nlayers)
    in_maps = []
    for b in range(cores):
        m = dict(sh)
        m["xT"] = np.ascontiguousarray(inp["x"][b].T)
        m["ctxT"] = np.ascontiguousarray(inp["ctx"][b].T)
        cv = np.stack([_cols(inp["c"][b]), _cols(inp["c_ctx"])], 2)
        m["cvec"] = np.ascontiguousarray(cv).reshape(128, 32)
        in_maps.append(m)
    res = run_bass_kernel_spmd(nc, in_maps, core_ids=list(range(cores)))
    out = np.stack([np.ascontiguousarray(r["outT"].T) for r in res.results], 0)
    return out.astype(np.float32)


def kernel(**inputs):
    return run(inputs, 4, 8)
```
